# Optimizing a Trainium2 kernel written in Bass

```python
import jax, jax.numpy as jnp
from jax import lax
import numpy as np

D_MODEL = 1024
BATCH = 4
SEQ = 8192
DEPTH = 1
DEC_BATCH = 16
DEC_SEQ = 4096
PAST_LEN = 128

D_MIX = D_MODEL
D_ATTN = D_MIX // 2
D_GMLP = D_MIX - D_ATTN
HEAD_DIM = 64
N_Q_HEADS = D_ATTN // HEAD_DIM
N_KV_HEADS = 2
Q_PER_KV = N_Q_HEADS // N_KV_HEADS
ROT_DIM = HEAD_DIM // 4
ROPE_THETA = 500000.0
WINDOW = 128
BLOCK = 128
N_GMLP_GROUPS = 8
GMLP_GROUP_DIM = D_GMLP // N_GMLP_GROUPS
CHUNK = 128
D_FF = 2816
CONV_W = 3
EPS = 1e-6
D_Q = N_Q_HEADS * HEAD_DIM
D_KV = N_KV_HEADS * HEAD_DIM
D_IN_PROJ = D_Q + 2 * D_KV + 2 * D_GMLP

kernel_name = "hymba_gmlp_swa_convffn_encoder"


def _rmsnorm(x, g):
    xf = x.astype(jnp.float32)
    y = xf * lax.rsqrt(jnp.mean(xf * xf, axis=-1, keepdims=True) + EPS)
    return (y * g.astype(jnp.float32)).astype(x.dtype)


def _layernorm(x, g, b):
    xf = x.astype(jnp.float32)
    mu = jnp.mean(xf, axis=-1, keepdims=True)
    var = jnp.mean(jnp.square(xf - mu), axis=-1, keepdims=True)
    y = (xf - mu) * lax.rsqrt(var + EPS)
    return (y * g.astype(jnp.float32) + b.astype(jnp.float32)).astype(x.dtype)


def _partial_rope(x):
    L = x.shape[1]
    half = ROT_DIM // 2
    inv_freq = ROPE_THETA ** (-jnp.arange(0, ROT_DIM, 2, dtype=jnp.float32) / ROT_DIM)
    ang = jnp.arange(L, dtype=jnp.float32)[:, None] * inv_freq[None, :]
    cos = jnp.cos(ang)[None, :, None, :]
    sin = jnp.sin(ang)[None, :, None, :]
    xr = x[..., :ROT_DIM].astype(jnp.float32)
    x1, x2 = xr[..., :half], xr[..., half:]
    rot = jnp.concatenate([x1 * cos - x2 * sin, x2 * cos + x1 * sin], axis=-1)
    return jnp.concatenate([rot.astype(x.dtype), x[..., ROT_DIM:]], axis=-1)


def _windowed_gqa(q, k, v, sink):
    B, L = q.shape[0], q.shape[1]
    nb = L // BLOCK
    qb = q.reshape(B, nb, BLOCK, N_KV_HEADS, Q_PER_KV, HEAD_DIM)
    pad = ((0, 0), (1, 1), (0, 0), (0, 0), (0, 0))
    kp = jnp.pad(k.reshape(B, nb, BLOCK, N_KV_HEADS, HEAD_DIM), pad)
    vp = jnp.pad(v.reshape(B, nb, BLOCK, N_KV_HEADS, HEAD_DIM), pad)
    kb = jnp.concatenate([kp[:, :-2], kp[:, 1:-1], kp[:, 2:]], axis=2)
    vb = jnp.concatenate([vp[:, :-2], vp[:, 1:-1], vp[:, 2:]], axis=2)
    scale = HEAD_DIM ** -0.5
    scores = jnp.einsum('bnqkgd,bnskd->bnkgqs', qb, kb,
                        preferred_element_type=jnp.float32) * scale
    a = jnp.arange(BLOCK)[:, None]
    s = jnp.arange(3 * BLOCK)[None, :]
    band = jnp.abs(s - BLOCK - a) <= WINDOW
    kpos = jnp.arange(nb)[:, None] * BLOCK - BLOCK + jnp.arange(3 * BLOCK)[None, :]
    inside = (kpos >= 0) & (kpos < L)
    mask = band[None, :, :] & inside[:, None, :]
    scores = jnp.where(mask[None, :, None, None, :, :], scores, jnp.finfo(jnp.float32).min)
    sink_b = jnp.broadcast_to(sink.astype(jnp.float32).reshape(1, 1, N_KV_HEADS, Q_PER_KV, 1, 1),
                              scores.shape[:-1] + (1,))
    p = jax.nn.softmax(jnp.concatenate([scores, sink_b], axis=-1), axis=-1)[..., :-1]
    out = jnp.einsum('bnkgqs,bnskd->bnqkgd', p.astype(v.dtype), vb)
    return out.reshape(B, L, D_ATTN)


def _chunked_gmlp(u, v, ln_g, ln_b, w_s, b_s):
    B, L = u.shape[0], u.shape[1]
    nc = L // CHUNK
    u = jax.nn.gelu(u, approximate=False)
    v = _layernorm(jax.nn.gelu(v, approximate=False), ln_g, ln_b)
    vc = v.reshape(B, nc, CHUNK, N_GMLP_GROUPS, GMLP_GROUP_DIM)
    sg = jnp.einsum('gij,bcjgd->bcigd', w_s, vc) + b_s.T[None, None, :, :, None]
    out = u.reshape(B, nc, CHUNK, N_GMLP_GROUPS, GMLP_GROUP_DIM) * sg
    return out.reshape(B, L, D_GMLP)


def _dwconv3(h, w, b):
    hp = jnp.pad(h, ((0, 0), (1, 1), (0, 0)))
    return hp[:, :-2] * w[0] + hp[:, 1:-1] * w[1] + hp[:, 2:] * w[2] + b


def _layer(x, norm_mix_pre, w_in, attn_sink, gmlp_ln_g, gmlp_ln_b, gmlp_w_s, gmlp_b_s,
           out_norm_attn, out_norm_gmlp, w_o, norm_mix_post, norm_ffn_pre, w_ffn_in,
           conv_w, conv_b, w_ffn_out, norm_ffn_post):
    B, L, _ = x.shape
    h = _rmsnorm(x, norm_mix_pre)
    p = h @ w_in
    o1 = D_Q
    o2 = o1 + D_KV
    o3 = o2 + D_KV
    o4 = o3 + D_GMLP
    q = _partial_rope(p[..., :o1].reshape(B, L, N_Q_HEADS, HEAD_DIM))
    k = _partial_rope(p[..., o1:o2].reshape(B, L, N_KV_HEADS, HEAD_DIM))
    v = p[..., o2:o3].reshape(B, L, N_KV_HEADS, HEAD_DIM)
    attn = _windowed_gqa(q, k, v, attn_sink)
    gm = _chunked_gmlp(p[..., o3:o4], p[..., o4:], gmlp_ln_g, gmlp_ln_b, gmlp_w_s, gmlp_b_s)
    mixed = jnp.concatenate([_rmsnorm(attn, out_norm_attn), _rmsnorm(gm, out_norm_gmlp)], axis=-1)
    x = x + _rmsnorm(mixed @ w_o, norm_mix_post)
    h = _rmsnorm(x, norm_ffn_pre)
    up = _dwconv3(h @ w_ffn_in, conv_w, conv_b)
    f = jax.nn.silu(up[..., :D_FF]) * up[..., D_FF:]
    x = x + _rmsnorm(f @ w_ffn_out, norm_ffn_post)
    return x


def _trunk(x, norm_mix_pre, w_in, attn_sink, gmlp_ln_g, gmlp_ln_b, gmlp_w_s, gmlp_b_s,
           out_norm_attn, out_norm_gmlp, w_o, norm_mix_post, norm_ffn_pre, w_ffn_in,
           conv_w, conv_b, w_ffn_out, norm_ffn_post):
    for l in range(DEPTH):
        x = _layer(x, norm_mix_pre[l], w_in[l], attn_sink[l], gmlp_ln_g[l], gmlp_ln_b[l],
                   gmlp_w_s[l], gmlp_b_s[l], out_norm_attn[l], out_norm_gmlp[l], w_o[l],
                   norm_mix_post[l], norm_ffn_pre[l], w_ffn_in[l], conv_w[l], conv_b[l],
                   w_ffn_out[l], norm_ffn_post[l])
    return x


def setup_inputs(seed: int = 0) -> dict:
    key = jax.random.key(seed)
    ks = jax.random.split(key, 20)
    f32 = jnp.float32

    def nrm(k, shape, scale):
        return jax.random.normal(k, shape, f32) * scale

    def gain(k, shape):
        return 1.0 + 0.05 * jax.random.normal(k, shape, f32)

    return {
        "x_prompt": nrm(ks[0], (BATCH, SEQ, D_MODEL), 1.0),
        "x_sample": nrm(ks[1], (DEC_BATCH, DEC_SEQ, D_MODEL), 1.0),
        "norm_mix_pre": gain(ks[2], (DEPTH, D_MODEL)),
        "w_in": nrm(ks[3], (DEPTH, D_MODEL, D_IN_PROJ), D_MODEL ** -0.5),
        "attn_sink": nrm(ks[4], (DEPTH, N_Q_HEADS), 0.5),
        "gmlp_ln_g": gain(ks[5], (DEPTH, D_GMLP)),
        "gmlp_ln_b": nrm(ks[6], (DEPTH, D_GMLP), 0.02),
        "gmlp_w_s": nrm(ks[7], (DEPTH, N_GMLP_GROUPS, CHUNK, CHUNK), CHUNK ** -0.5),
        "gmlp_b_s": 1.0 + nrm(ks[8], (DEPTH, N_GMLP_GROUPS, CHUNK), 0.02),
        "out_norm_attn": gain(ks[9], (DEPTH, D_ATTN)),
        "out_norm_gmlp": gain(ks[10], (DEPTH, D_GMLP)),
        "w_o": nrm(ks[11], (DEPTH, D_MIX, D_MODEL), D_MIX ** -0.5),
        "norm_mix_post": gain(ks[12], (DEPTH, D_MODEL)),
        "norm_ffn_pre": gain(ks[13], (DEPTH, D_MODEL)),
        "w_ffn_in": nrm(ks[14], (DEPTH, D_MODEL, 2 * D_FF), D_MODEL ** -0.5),
        "conv_w": nrm(ks[15], (DEPTH, CONV_W, 2 * D_FF), CONV_W ** -0.5),
        "conv_b": nrm(ks[16], (DEPTH, 2 * D_FF), 0.02),
        "w_ffn_out": nrm(ks[17], (DEPTH, D_FF, D_MODEL), D_FF ** -0.5),
        "norm_ffn_post": gain(ks[18], (DEPTH, D_MODEL)),
    }


def reference(x_prompt, x_sample, norm_mix_pre, w_in, attn_sink, gmlp_ln_g, gmlp_ln_b,
              gmlp_w_s, gmlp_b_s, out_norm_attn, out_norm_gmlp, w_o, norm_mix_post,
              norm_ffn_pre, w_ffn_in, conv_w, conv_b, w_ffn_out, norm_ffn_post):
    y_prompt = _trunk(x_prompt, norm_mix_pre, w_in, attn_sink, gmlp_ln_g, gmlp_ln_b,
                      gmlp_w_s, gmlp_b_s, out_norm_attn, out_norm_gmlp, w_o, norm_mix_post,
                      norm_ffn_pre, w_ffn_in, conv_w, conv_b, w_ffn_out, norm_ffn_post)
    y_sample = _trunk(x_sample, norm_mix_pre, w_in, attn_sink, gmlp_ln_g, gmlp_ln_b,
                      gmlp_w_s, gmlp_b_s, out_norm_attn, out_norm_gmlp, w_o, norm_mix_post,
                      norm_ffn_pre, w_ffn_in, conv_w, conv_b, w_ffn_out, norm_ffn_post)
    return (y_prompt, y_sample)
```

```python
import numpy as np
from contextlib import ExitStack
import ml_dtypes
import concourse.bass as bass
import concourse.mybir as mybir
from concourse.bass_utils import run_bass_kernel_spmd

F32 = mybir.dt.float32
BF16 = mybir.dt.bfloat16
AF = mybir.ActivationFunctionType
ALU = mybir.AluOpType

ENGS = ("pe", "act", "dve", "pool", "sp")
EPS = 1e-6
D = 1024
DFF = 2816
NPAIR = 22
DIN = 1792


class CALL:
    __slots__ = ("meth", "args", "kw")

    def __init__(self, meth, *args, **kw):
        self.meth = meth
        self.args = args
        self.kw = kw

    def __call__(self, engine):
        return getattr(engine, self.meth)(*self.args, **self.kw)


class Op:
    __slots__ = ("eng", "fn", "deps", "signal", "rank", "dma_key", "dma_val")

    def __init__(self, eng, fn, dma_key=None):
        self.eng = eng
        self.fn = fn
        self.deps = []
        self.signal = False
        self.rank = 0
        self.dma_key = dma_key
        self.dma_val = 0


class Prog:
    def __init__(self):
        self.ops = {e: [] for e in ENGS}
        self.lastw = {}
        self.readers = {}
        self.dma_counts = {}

    def add(self, eng, fn, reads=(), writes=(), dma_key=None, nodep=False):
        op = Op(eng, fn, dma_key)
        px = [k for k in reads if isinstance(k, tuple) and k[0] in ("pm", "pt")]
        if px:
            writes = list(writes) + px
        deps = {}
        if not nodep:
            for k in reads:
                w = self.lastw.get(k)
                if w is not None:
                    deps[id(w)] = w
            for k in writes:
                w = self.lastw.get(k)
                if w is not None:
                    deps[id(w)] = w
                for r in self.readers.get(k, {}).values():
                    deps[id(r)] = r
        for d in deps.values():
            if d.eng == "pe" and eng == "pe" and d.dma_key is None and dma_key is None:
                continue
            op.deps.append(d)
            d.signal = True
        rk = dma_key if dma_key is not None else eng
        for k in reads:
            self.readers.setdefault(k, {})[rk] = op
        for k in writes:
            self.lastw[k] = op
            self.readers[k] = {}
        if dma_key is not None:
            c = self.dma_counts.get(dma_key, 0) + 1
            self.dma_counts[dma_key] = c
            op.dma_val = 16 * c
        self.ops[eng].append(op)
        return op

    def emit(self, nc, stack):
        esem = {e: stack.enter_context(nc.semaphore("s_" + e)) for e in ENGS}
        dsem = {k: stack.enter_context(nc.semaphore("d_" + str(k))) for k in self.dma_counts}
        for e in ENGS:
            r = 0
            for op in self.ops[e]:
                if op.dma_key is None and op.signal:
                    r += 1
                    op.rank = r

        def run(eng_name):
            def body(engine):
                known = {}
                for op in self.ops[eng_name]:
                    for d in op.deps:
                        if d.dma_key is not None:
                            sem, val, key = dsem[d.dma_key], d.dma_val, ("d", d.dma_key)
                        else:
                            sem, val, key = esem[d.eng], d.rank, ("e", d.eng)
                        if known.get(key, 0) < val:
                            engine.wait_ge(sem, val)
                            known[key] = val
                    ins = op.fn(engine)
                    if op.dma_key is not None:
                        ins.then_inc(dsem[op.dma_key], 16)
                    elif op.signal:
                        ins.then_inc(esem[eng_name], 1)
            return body

        with nc.Block() as block:
            block.tensor(run("pe"))
            block.scalar(run("act"))
            block.vector(run("dve"))
            block.gpsimd(run("pool"))
            block.sync(run("sp"))


C_G1, C_GMIX, C_G2, C_SINK, C_BS = 0, 8, 16, 24, 32
C_CW = 40
C_FLAG = 216
C_NH = 217
C_LNG = 232
C_LNB = C_LNG + 512
C_GPOST = C_LNB + 512
C_G3 = C_GPOST + 1024
C_TOT = C_G3 + 1024


def window_sizes(UB, WB=2):
    w = []
    r = UB
    while r > 0:
        s = min(WB, r)
        w.append(s)
        r -= s
    return w


def build_program(UB, NX=5, NXT=2, NWF=2, WB=2, R_PAIR=1.0, R_OUT=4.0):
    NU = 3
    NBLK = NU * UB
    wsz = window_sizes(UB, WB)
    wins = []
    for u in range(NU):
        b0 = u * UB
        for s in wsz:
            wins.append((b0, s))
            b0 += s
    win_of = {}
    for wi, (b0, s) in enumerate(wins):
        for i in range(s):
            win_of[b0 + i] = (wi, i)
    NWIN = len(wins)
    HW = WB * 128 + 4

    nc = bass.Bass("TRN2", target_bir_lowering=False)
    dt_in = lambda name, shape, dt: nc.dram_tensor(name, list(shape), dt, kind="ExternalInput").ap()
    x_d = dt_in("x", [NBLK * 128, D], F32)
    win_d = dt_in("w_in", [128, 8, DIN], F32)
    wo_d = dt_in("w_o", [128, 8, D], F32)
    wfi_d = dt_in("w_fi", [128, 8, NPAIR, 256], F32)
    wd_d = dt_in("w_d", [128, NPAIR, D], F32)
    ws_d = dt_in("w_s", [128, 8, 128], F32)
    cp_d = dt_in("cpack", [128, C_TOT], F32)
    rope_d = dt_in("rope", [128, NBLK, 16], F32)
    cb_d = dt_in("cbf", [128, 640], BF16)
    y_d = nc.dram_tensor("y", [NBLK * 128, D], F32, kind="ExternalOutput").ap()
    scr = nc.dram_tensor("wfi_scr", [NPAIR, 128, 8, 256], BF16, kind="Internal").ap()

    P = Prog()
    with ExitStack() as st:
        def sb(name, shape, dt):
            return st.enter_context(nc.sbuf_tensor(name, list(shape), dt))

        cp = sb("cp", [128, C_TOT], F32)
        ropeR = [sb(f"rope{i}", [128, 16], F32) for i in range(4)]
        cb = sb("cb", [128, 640], BF16)
        mskc = sb("mskc", [128, 256], BF16)
        esink = sb("esink", [128, 8], F32)
        Win = sb("Win", [128, 8, DIN], BF16)
        Wo = sb("Wo", [128, 8, D], BF16)
        Wd = sb("Wd", [128, NPAIR, D], BF16)
        WsT = sb("WsT", [128, 8, 128], BF16)
        wfi = [sb(f"wfi{i}", [128, 8, 256], BF16) for i in range(NWF)]
        xs = [sb(f"xs{i}", [128, D], F32) for i in range(NX)]
        xa = [sb(f"xa{i}", [128, D], F32) for i in range(NXT)]
        xbf = sb("xbf", [128, D], BF16)
        junk = sb("junk", [128, D], BF16)
        xT = sb("xT", [128, 8, 128], BF16)
        qkbf = sb("qkbf", [128, 10, 64], BF16)
        r32 = sb("r32", [128, 10, 16], F32)
        rt = sb("rt", [128, 4, 10, 8], F32)
        NQ, NK, NG = 3, 4, 3
        QT = [sb(f"QT{i}", [128, 4, 128], BF16) for i in range(NQ)]
        KT = [sb(f"KT{i}", [128, 128], BF16) for i in range(NK)]
        Vx = [sb(f"Vx{i}", [128, 2, 66], BF16) for i in range(NK)]
        gu = sb("gu", [128, 512], F32)
        gv = sb("gv", [128, 512], F32)
        vln = sb("vln", [128, 512], BF16)
        mixg = [sb(f"mixg{i}", [128, 512], BF16) for i in range(NG)]
        st6 = sb("st6", [128, 6], F32)
        mv = sb("mv", [128, 2], F32)
        NS = 4
        sc = [sb(f"sc{i}", [128, 24], F32) for i in range(NS)]
        PT = [sb(f"PT{i}", [128, 512], BF16) for i in range(6)]
        a32 = sb("a32", [128, 512], F32)
        den = sb("den", [128, 8], F32)
        mixa = sb("mixa", [128, 512], BF16)
        mixT = sb("mixT", [128, 8, 128], BF16)
        tmpB = sb("tmpX", [128, D], F32)
        h2bf = sb("h2bf", [128, D], BF16)
        NH = 3
        h2T = [sb(f"h2T{i}", [128, 8, HW], BF16) for i in range(NH)]
        fT = sb("fT", [128, NPAIR, WB * 128], BF16)
        cbuf = sb("cbuf", [128, 4, 256], F32)
        ugs = [cbuf[:, i, :] for i in range(2)]
        uus = [cbuf[:, 2 + i, :] for i in range(2)]
        tmpC = cbuf[:].rearrange("p a c -> p (a c)")
        psT = [st.enter_context(nc.psum_tensor(f"psT{i}", [128, 1024], BF16)) for i in range(2)]
        NPM = 6
        psM = [st.enter_context(nc.psum_tensor(f"psM{i}", [128, 512], F32)) for i in range(NPM)]

        ident = cb[:, 0:128]
        maskP = cb[:, 128:256]
        maskN = cb[:, 256:384]
        maskPc = mskc[:, 0:128]
        maskNc = mskc[:, 128:256]
        flag = cp[:, C_FLAG:C_FLAG + 1]
        nhalf = cp[:, C_NH:C_NH + 1]

        cnt = {"pm": 0, "pt": 0, "wf": 0}

        def next_pm():
            i = cnt["pm"] % NPM
            cnt["pm"] += 1
            return i

        def next_pt():
            i = cnt["pt"] % 2
            cnt["pt"] += 1
            return i

        CON = "consts"

        P.add("sp", CALL("dma_start", out=cp[:], in_=cp_d), writes=[CON], dma_key="c0")
        P.add("sp", CALL("dma_start", out=cb[:], in_=cb_d), writes=["cb"], dma_key="c2")
        P.add("pool", CALL("dma_start", out=Wd[:], in_=wd_d), writes=["Wd"], dma_key="c3")
        P.add("pool", CALL("dma_start", out=WsT[:], in_=ws_d), writes=["WsT"], dma_key="c4")
        P.add("act", CALL("activation", out=esink[:], in_=cp[:, C_SINK:C_SINK + 8], func=AF.Exp),
              reads=[CON], writes=["esink"])
        for (dst_m, src_m) in ((maskPc, cb[:, 384:512]), (maskNc, cb[:, 512:640])):
            P.add("dve", CALL("tensor_scalar", out=a32[:, 0:128], in0=src_m, scalar1=flag, scalar2=-1.0, op0=ALU.mult, op1=ALU.add),
                  reads=[CON, "cb"], writes=[("a32", 0)])
            P.add("dve", CALL("tensor_scalar", out=dst_m, in0=a32[:, 0:128], scalar1=30000.0, scalar2=None, op0=ALU.mult),
                  reads=[("a32", 0)], writes=["mskc"])
        for i in range(NK):
            P.add("dve", CALL("memset", Vx[i][:, :, 64:66], 1.0), writes=[("Vx", i)])
        for i in range(NH):
            P.add("dve", CALL("memset", h2T[i][:], 0.0), writes=[("h2T", i, "L"), ("h2T", i, "R")])

        fTflat = fT[:].rearrange("p j t -> p (j t)")
        piece = {"i": 0}

        def stage_piece(src_ap, ncols, gcol, dst_ap, dst_keys, post=None):
            i = piece["i"]
            piece["i"] += 1
            s = i % NX
            P.add("sp", CALL("dma_start", out=xs[s][:, 0:ncols], in_=src_ap), writes=[("xs", s)], dma_key=f"xs{s}")
            P.add("dve", CALL("tensor_scalar", out=dst_ap, in0=xs[s][:, 0:ncols], scalar1=cp[:, gcol:gcol + 1],
                              scalar2=None, op0=ALU.mult),
                  reads=[("xs", s), CON], writes=dst_keys)
            if post is not None:
                post()

        for k in range(8):
            stage_piece(win_d[:, k, 0:1024], 1024, C_G1 + k, Win[:, k, 0:1024], ["Win"])
            stage_piece(win_d[:, k, 1024:DIN], DIN - 1024, C_G1 + k, Win[:, k, 1024:DIN], ["Win"])
            stage_piece(wo_d[:, k, :], 1024, C_GMIX + k, Wo[:, k, :], ["Wo"])
        NSTG = 3
        sidx = 0
        for k in range(8):
            for jg in range(6):
                j0 = jg * 4
                nj = min(4, NPAIR - j0)
                ncols = nj * 256
                sl = sidx % NSTG
                sidx += 1
                stg = fTflat[:, sl * 1024: sl * 1024 + ncols]
                src = wfi_d[:, k, j0:j0 + nj, :].rearrange("p j c -> p (j c)")
                dst = scr[j0:j0 + nj, :, k, :].rearrange("j p c -> p j c")

                def post(stg=stg, dst=dst, sl=sl, jg=jg, k=k, nj=nj):
                    P.add("sp", CALL("dma_start", out=dst, in_=stg.rearrange("p (j c) -> p j c", j=nj)),
                          reads=[("stg", sl)], writes=[("scr", jg, k)], dma_key=f"stg{sl}")
                stage_piece(src, ncols, C_G2 + k, stg, [("stg", sl)], post)
        SCR_KEYS = [("scr", jg, k) for jg in range(6) for k in range(8)]

        def rsqrt_pool(out_ap, in_ap, scale, keys_r, keys_w):
            P.add("pool", CALL("tensor_scalar", out=out_ap, in0=in_ap, scalar1=scale, scalar2=EPS,
                               op0=ALU.mult, op1=ALU.add), reads=keys_r, writes=keys_w)
            P.add("pool", CALL("tensor_tensor", out=out_ap, in0=out_ap, in1=nhalf, op=ALU.pow),
                  reads=keys_w + [CON], writes=keys_w)

        owner = [None] * NX
        slot_of = {}

        def load_A(b):
            t = b % NXT
            P.add("sp", CALL("dma_start", out=xa[t][:], in_=x_d[b * 128:(b + 1) * 128, :]),
                  writes=[("xa", t)], dma_key=f"xa{t}")
            P.add("sp", CALL("dma_start", out=ropeR[b % 4][:], in_=rope_d[:, b, :]),
                  writes=[("rope", b % 4)], dma_key=f"rp{b % 4}")

        def alloc_B(b):
            s = None
            for i in range(NX):
                if owner[i] is None:
                    s = i
                    break
            if s is None:
                return False
            owner[s] = b
            slot_of[b] = s
            return True

        def reload_x(b):
            s = slot_of[b]
            P.add("sp", CALL("dma_start", out=xs[s][:], in_=x_d[b * 128:(b + 1) * 128, :]),
                  writes=[("xs", s)], dma_key=f"xs{s}")

        def unit_pos(b):
            return b // UB, b % UB

        def gen_A(b):
            X = xa[b % NXT]
            xk = ("xa", b % NXT)
            S = sc[b % NS]
            sk = lambda c: ("sc", b % NS, c)
            P.add("act", CALL("activation", out=junk[:], in_=X[:], func=AF.Square, accum_out=S[:, 0:1]),
                  reads=[xk], writes=[sk(0), "junk"])
            rsqrt_pool(S[:, 1:2], S[:, 0:1], 1.0 / D, [sk(0)], [sk(1)])
            rstd1 = S[:, 1:2]
            P.add("act", CALL("activation", out=xbf[:], in_=X[:], func=AF.Copy), reads=[xk], writes=["xbf"])
            yield
            pt = next_pt()
            for k in range(8):
                P.add("pe", CALL("transpose", out=psT[pt][:, k * 128:(k + 1) * 128],
                                 in_=xbf[:, k * 128:(k + 1) * 128], identity=ident),
                      reads=["xbf", "cb"], writes=[("pt", pt)])
            P.add("act", CALL("copy", out=xT[:].rearrange("p k t -> p (k t)"), in_=psT[pt][:]),
                  reads=[("pt", pt)], writes=["xT"])
            yield
            jobs = [(0, 512), (512, 256), (768, 512), (1280, 512)]
            pms = []
            for (c0, n) in jobs:
                pm = next_pm()
                pms.append(pm)
                for k in range(8):
                    P.add("pe", CALL("matmul", psM[pm][:, 0:n], lhsT=xT[:, k, :], rhs=Win[:, k, c0:c0 + n],
                                     start=(k == 0), stop=(k == 7)),
                          reads=["xT", "Win"], writes=[("pm", pm)])
            pq, pkv, pu, pv = pms
            P.add("act", CALL("activation", out=qkbf[:, 0:8, :].rearrange("p h d -> p (h d)"), in_=psM[pq][:, 0:512],
                              func=AF.Identity, scale=rstd1),
                  reads=[("pm", pq), sk(1)], writes=["qkbf"])
            P.add("dve", CALL("tensor_scalar", out=r32[:, 0:8, :],
                              in0=psM[pq][:, 0:512].rearrange("p (h d) -> p h d", d=64)[:, :, 0:16],
                              scalar1=rstd1, scalar2=None, op0=ALU.mult),
                  reads=[("pm", pq), sk(1)], writes=["r32"])
            P.add("act", CALL("activation", out=qkbf[:, 8:10, :].rearrange("p h d -> p (h d)"), in_=psM[pkv][:, 0:128],
                              func=AF.Identity, scale=rstd1),
                  reads=[("pm", pkv), sk(1)], writes=["qkbf"])
            vs = b % NK
            P.add("dve", CALL("tensor_scalar", out=Vx[vs][:, :, 0:64],
                              in0=psM[pkv][:, 128:256].rearrange("p (a d) -> p a d", d=64),
                              scalar1=rstd1, scalar2=None, op0=ALU.mult),
                  reads=[("pm", pkv), sk(1)], writes=[("Vx", vs)])
            P.add("dve", CALL("tensor_scalar", out=r32[:, 8:10, :],
                              in0=psM[pkv][:, 0:128].rearrange("p (h d) -> p h d", d=64)[:, :, 0:16],
                              scalar1=rstd1, scalar2=None, op0=ALU.mult),
                  reads=[("pm", pkv), sk(1)], writes=["r32"])
            P.add("act", CALL("activation", out=gu[:], in_=psM[pu][:], func=AF.Gelu, scale=rstd1),
                  reads=[("pm", pu), sk(1)], writes=["gu"])
            P.add("act", CALL("activation", out=gv[:], in_=psM[pv][:], func=AF.Gelu, scale=rstd1),
                  reads=[("pm", pv), sk(1)], writes=["gv"])
            P.add("dve", CALL("bn_stats", out=st6[:], in_=gv[:]), reads=["gv"], writes=["st6"])
            P.add("dve", CALL("bn_aggr", out=mv[:], in_=st6[:]), reads=["st6"], writes=["mv"])
            rsqrt_pool(S[:, 2:3], mv[:, 1:2], 1.0, ["mv"], [sk(2)])
            P.add("dve", CALL("scalar_tensor_tensor", out=gv[:], in0=gv[:], scalar=mv[:, 0:1],
                              in1=cp[:, C_LNG:C_LNG + 512], op0=ALU.subtract, op1=ALU.mult),
                  reads=["gv", "mv", CON], writes=["gv"])
            P.add("dve", CALL("scalar_tensor_tensor", out=vln[:], in0=gv[:], scalar=S[:, 2:3],
                              in1=cp[:, C_LNB:C_LNB + 512], op0=ALU.mult, op1=ALU.add),
                  reads=["gv", sk(2), CON], writes=["vln"])
            rk = ("rope", b % 4)
            cosb = ropeR[b % 4][:, 0:8].unsqueeze(1).broadcast_to([128, 10, 8])
            sinb = ropeR[b % 4][:, 8:16].unsqueeze(1).broadcast_to([128, 10, 8])
            x1 = r32[:, :, 0:8]
            x2 = r32[:, :, 8:16]
            P.add("dve", CALL("tensor_tensor", out=rt[:, 0], in0=x1, in1=cosb, op=ALU.mult), reads=["r32", rk], writes=["rt0"])
            P.add("dve", CALL("tensor_tensor", out=rt[:, 1], in0=x2, in1=sinb, op=ALU.mult), reads=["r32", rk], writes=["rt1"])
            P.add("dve", CALL("tensor_tensor", out=rt[:, 2], in0=x2, in1=cosb, op=ALU.mult), reads=["r32", rk], writes=["rt2"])
            P.add("dve", CALL("tensor_tensor", out=rt[:, 3], in0=x1, in1=sinb, op=ALU.mult), reads=["r32", rk], writes=["rt3"])
            P.add("dve", CALL("tensor_tensor", out=qkbf[:, :, 0:8], in0=rt[:, 0], in1=rt[:, 1], op=ALU.subtract),
                  reads=["rt0", "rt1"], writes=["qkbf"])
            P.add("dve", CALL("tensor_tensor", out=qkbf[:, :, 8:16], in0=rt[:, 2], in1=rt[:, 3], op=ALU.add),
                  reads=["rt2", "rt3"], writes=["qkbf"])
            yield
            yield
            pg = next_pm()
            for g in range(8):
                P.add("pe", CALL("matmul", psM[pg][:, g * 64:(g + 1) * 64], lhsT=WsT[:, g, :],
                                 rhs=vln[:, g * 64:(g + 1) * 64], start=True, stop=True),
                      reads=["vln", "WsT"], writes=[("pm", pg)])
            bsb = cp[:, C_BS:C_BS + 8].unsqueeze(2).broadcast_to([128, 8, 64])
            P.add("dve", CALL("tensor_tensor", out=gv[:].rearrange("p (g d) -> p g d", d=64),
                              in0=psM[pg][:].rearrange("p (g d) -> p g d", d=64), in1=bsb, op=ALU.add),
                  reads=[("pm", pg), CON], writes=["gv"])
            P.add("pool", CALL("tensor_tensor", out=gv[:], in0=gv[:], in1=gu[:], op=ALU.mult),
                  reads=["gv", "gu"], writes=["gv"])
            P.add("act", CALL("activation", out=junk[:, 0:512], in_=gv[:], func=AF.Square, accum_out=S[:, 3:4]),
                  reads=["gv"], writes=[sk(3), "junk"])
            rsqrt_pool(S[:, 4:5], S[:, 3:4], 1.0 / 512, [sk(3)], [sk(4)])
            ms = b % NG
            P.add("pool", CALL("tensor_scalar", out=mixg[ms][:], in0=gv[:], scalar1=S[:, 4:5], scalar2=1.0,
                               op0=ALU.mult, op1=ALU.mult),
                  reads=["gv", sk(4)], writes=[("mixg", ms)])
            pt2 = next_pt()
            qkflat = qkbf[:].rearrange("p h d -> p (h d)")
            for c in range(5):
                P.add("pe", CALL("transpose", out=psT[pt2][:, c * 128:(c + 1) * 128],
                                 in_=qkflat[:, c * 128:(c + 1) * 128], identity=ident),
                      reads=["qkbf", "cb"], writes=[("pt", pt2)])
            qs = b % NQ
            P.add("act", CALL("copy", out=QT[qs][:].rearrange("p c t -> p (c t)"), in_=psT[pt2][:, 0:512]),
                  reads=[("pt", pt2)], writes=[("QT", qs)])
            P.add("act", CALL("copy", out=KT[vs][:], in_=psT[pt2][:, 512:640]),
                  reads=[("pt", pt2)], writes=[("KT", vs)])
            yield

        h2own = [None] * NH

        def h2claim(buf, wi):
            assert h2own[buf] in (None, wi), f"h2T buffer {buf} owned by window {h2own[buf]}, wanted by {wi}"
            h2own[buf] = wi

        def gen_B(b):
            u, pb = unit_pos(b)
            s = slot_of[b]
            X = xs[s]
            S = sc[b % NS]
            sk = lambda c: ("sc", b % NS, c)
            qs = b % NQ
            kbs = []
            if pb > 0:
                kbs.append((b - 1, maskP))
            elif u == 1:
                kbs.append((b - 1, maskPc))
            kbs.append((b, None))
            if pb < UB - 1:
                kbs.append((b + 1, maskN))
            elif u == 0:
                kbs.append((b + 1, maskNc))
            ptsl = {}
            for kvh in range(2):
                lo, hi = 64 * kvh, 64 * kvh + 64
                pts = []
                for i, (kb, msk) in enumerate(kbs):
                    pm = next_pm()
                    ks = kb % NK
                    P.add("pe", CALL("matmul", psM[pm][:], lhsT=KT[ks][lo:hi, :], rhs=QT[qs][lo:hi, :, :],
                                     start=True, stop=(msk is None)),
                          reads=[("KT", ks), ("QT", qs)], writes=[("pm", pm)])
                    if msk is not None:
                        P.add("pe", CALL("matmul", psM[pm][:].rearrange("p (g q) -> p g q", g=4), lhsT=ident,
                                         rhs=msk.unsqueeze(1).broadcast_to([128, 4, 128]), start=False, stop=True),
                              reads=["cb", "mskc"], writes=[("pm", pm)])
                    pi = kvh * 3 + i
                    pts.append(pi)
                    P.add("act", CALL("activation", out=PT[pi][:], in_=psM[pm][:], func=AF.Exp, scale=0.125),
                          reads=[("pm", pm)], writes=[("PT", pi)])
                ptsl[kvh] = pts
                yield
            for kvh in range(2):
                pts = ptsl[kvh]
                if kvh == 1:
                    reload_x(b)
                po = next_pm()
                O = psM[po][:, 0:260].rearrange("p (g d) -> p g d", d=65)
                for g in range(4):
                    for i, (kb, msk) in enumerate(kbs):
                        ks = kb % NK
                        pi = pts[i]
                        P.add("pe", CALL("matmul", O[:, g, :], lhsT=PT[pi][:, g * 128:(g + 1) * 128],
                                         rhs=Vx[ks][:, kvh, 0:65], start=(i == 0), stop=(i == len(kbs) - 1)),
                              reads=[("PT", pi), ("Vx", ks)], writes=[("pm", po)])
                dk = den[:, kvh * 4:(kvh + 1) * 4]
                P.add("dve", CALL("tensor_tensor", out=dk.unsqueeze(2), in0=O[:, :, 64:65],
                                  in1=esink[:, kvh * 4:(kvh + 1) * 4].unsqueeze(2), op=ALU.add),
                      reads=[("pm", po), "esink"], writes=[("den", kvh)])
                P.add("dve", CALL("reciprocal", out=dk, in_=dk), reads=[("den", kvh)], writes=[("den", kvh)])
                P.add("dve", CALL("tensor_tensor",
                                  out=a32[:, kvh * 256:(kvh + 1) * 256].rearrange("p (g d) -> p g d", d=64),
                                  in0=O[:, :, 0:64], in1=dk.unsqueeze(2).broadcast_to([128, 4, 64]), op=ALU.mult),
                      reads=[("pm", po), ("den", kvh)], writes=[("a32", kvh)])
                if kvh == 1:
                    P.add("act", CALL("activation", out=junk[:, 0:512], in_=a32[:], func=AF.Square, accum_out=S[:, 5:6]),
                          reads=[("a32", 0), ("a32", 1)], writes=[sk(5), "junk"])
                    rsqrt_pool(S[:, 6:7], S[:, 5:6], 1.0 / 512, [sk(5)], [sk(6)])
                    P.add("pool", CALL("tensor_scalar", out=mixa[:], in0=a32[:], scalar1=S[:, 6:7], scalar2=1.0,
                                       op0=ALU.mult, op1=ALU.mult),
                          reads=[("a32", 0), ("a32", 1), sk(6)], writes=["mixa"])
                    yield
                yield
            ms = b % NG
            ptm = next_pt()
            for c in range(4):
                P.add("pe", CALL("transpose", out=psT[ptm][:, c * 128:(c + 1) * 128],
                                 in_=mixa[:, c * 128:(c + 1) * 128], identity=ident),
                      reads=["mixa", "cb"], writes=[("pt", ptm)])
            for c in range(4):
                P.add("pe", CALL("transpose", out=psT[ptm][:, (4 + c) * 128:(5 + c) * 128],
                                 in_=mixg[ms][:, c * 128:(c + 1) * 128], identity=ident),
                      reads=[("mixg", ms), "cb"], writes=[("pt", ptm)])
            P.add("act", CALL("copy", out=mixT[:].rearrange("p k t -> p (k t)"), in_=psT[ptm][:]),
                  reads=[("pt", ptm)], writes=["mixT"])
            yield
            pmm = []
            for n in range(2):
                pm = next_pm()
                pmm.append(pm)
                for k in range(8):
                    P.add("pe", CALL("matmul", psM[pm][:], lhsT=mixT[:, k, :], rhs=Wo[:, k, n * 512:(n + 1) * 512],
                                     start=(k == 0), stop=(k == 7)),
                          reads=["mixT", "Wo"], writes=[("pm", pm)])
                P.add("act", CALL("activation", out=junk[:, 0:512], in_=psM[pm][:], func=AF.Square,
                                  accum_out=S[:, 7 + n:8 + n]),
                      reads=[("pm", pm)], writes=[sk(7 + n), "junk"])
            P.add("pool", CALL("tensor_tensor", out=S[:, 9:10], in0=S[:, 7:8], in1=S[:, 8:9], op=ALU.add),
                  reads=[sk(7), sk(8)], writes=[sk(9)])
            rsqrt_pool(S[:, 10:11], S[:, 9:10], 1.0 / D, [sk(9)], [sk(10)])
            for n in range(2):
                pm = pmm[n]
                P.add("dve", CALL("tensor_tensor", out=tmpB[:, n * 512:(n + 1) * 512], in0=psM[pm][:],
                                  in1=cp[:, C_GPOST + n * 512:C_GPOST + (n + 1) * 512], op=ALU.mult),
                      reads=[("pm", pm), CON], writes=[("tmpX", n)])
            for n in range(2):
                P.add("dve", CALL("scalar_tensor_tensor", out=X[:, n * 512:(n + 1) * 512], in0=tmpB[:, n * 512:(n + 1) * 512],
                                  scalar=S[:, 10:11], in1=X[:, n * 512:(n + 1) * 512], op0=ALU.mult, op1=ALU.add),
                      reads=[("tmpX", n), sk(10), ("xs", s)], writes=[("xs", s)])
            P.add("act", CALL("activation", out=junk[:], in_=X[:], func=AF.Square, accum_out=S[:, 11:12]),
                  reads=[("xs", s)], writes=[sk(11), "junk"])
            rsqrt_pool(S[:, 12:13], S[:, 11:12], 1.0 / D, [sk(11)], [sk(12)])
            P.add("act", CALL("activation", out=h2bf[:], in_=X[:], func=AF.Identity, scale=S[:, 12:13]),
                  reads=[("xs", s), sk(12)], writes=["h2bf"])
            yield
            yield
            yield
            pth = next_pt()
            for k in range(8):
                P.add("pe", CALL("transpose", out=psT[pth][:, k * 128:(k + 1) * 128],
                                 in_=h2bf[:, k * 128:(k + 1) * 128], identity=ident),
                      reads=["h2bf", "cb"], writes=[("pt", pth)])
            wi, bi = win_of[b]
            nb = wins[wi][1]
            hb = wi % NH
            h2claim(hb, wi)
            src = psT[pth][:].rearrange("p (k t) -> p k t", t=128)
            P.add("act", CALL("copy", out=h2T[hb][:, :, 2 + bi * 128:2 + (bi + 1) * 128], in_=src),
                  reads=[("pt", pth)], writes=[("h2T", hb, bi)])
            if bi == 0 and wi > 0:
                ph = (wi - 1) % NH
                h2claim(ph, wi - 1)
                rc = 2 + wins[wi - 1][1] * 128
                if pb > 0:
                    P.add("dve", CALL("tensor_copy", out=h2T[ph][:, :, rc:rc + 1], in_=src[:, :, 0:1]),
                          reads=[("pt", pth)], writes=[("h2T", ph, "R")])
                elif u == 1:
                    P.add("dve", CALL("tensor_scalar", out=h2T[ph][:, :, rc:rc + 1], in0=src[:, :, 0:1], scalar1=flag,
                                      scalar2=None, op0=ALU.mult),
                          reads=[("pt", pth), CON], writes=[("h2T", ph, "R")])
                else:
                    P.add("dve", CALL("memset", h2T[ph][:, :, rc:rc + 1], 0.0), writes=[("h2T", ph, "R")])
            if bi == nb - 1 and wi < NWIN - 1:
                nh = (wi + 1) % NH
                h2claim(nh, wi + 1)
                if pb < UB - 1:
                    P.add("dve", CALL("tensor_copy", out=h2T[nh][:, :, 1:2], in_=src[:, :, 127:128]),
                          reads=[("pt", pth)], writes=[("h2T", nh, "L")])
                elif u == 0:
                    P.add("dve", CALL("tensor_scalar", out=h2T[nh][:, :, 1:2], in0=src[:, :, 127:128], scalar1=flag,
                                      scalar2=None, op0=ALU.mult),
                          reads=[("pt", pth), CON], writes=[("h2T", nh, "L")])
                else:
                    P.add("dve", CALL("memset", h2T[nh][:, :, 1:2], 0.0), writes=[("h2T", nh, "L")])
            if wi == NWIN - 1 and bi == nb - 1:
                rc = 2 + nb * 128
                P.add("dve", CALL("memset", h2T[hb][:, :, rc:rc + 1], 0.0), writes=[("h2T", hb, "R")])
            yield

        def cwc(which, half, j):
            c = C_CW + (which * 2 + half) * NPAIR + j
            return cp[:, c:c + 1]

        wl = {"n": 0}

        def wload(k):
            while wl["n"] <= k and wl["n"] < NWIN * NPAIR:
                kk = wl["n"]
                wl["n"] += 1
                P.add("sp", CALL("dma_start", out=wfi[kk % NWF][:], in_=scr[kk % NPAIR]),
                      reads=SCR_KEYS, writes=[("wfi", kk % NWF)], dma_key=f"wf{kk % NWF}")

        def gen_C(wi):
            b0, nb = wins[wi]
            hb = wi % NH
            assert h2own[hb] == wi
            n = nb * 128
            N = n + 2
            hkeys = [("h2T", hb, i) for i in range(nb)] + [("h2T", hb, "L"), ("h2T", hb, "R")]
            for j in range(NPAIR):
                kglob = wi * NPAIR + j
                wload(kglob)
                wfs = kglob % NWF
                pmg = next_pm()
                pmu = next_pm()
                for half, pm in ((0, pmg), (1, pmu)):
                    for k in range(8):
                        P.add("pe", CALL("matmul", psM[pm][:, 0:N], lhsT=wfi[wfs][:, k, half * 128:(half + 1) * 128],
                                         rhs=h2T[hb][:, k, 1:N + 1], start=(k == 0), stop=(k == 7)),
                              reads=[("wfi", wfs)] + hkeys, writes=[("pm", pm)])
                ug, uu = ugs[j % 2], uus[j % 2]
                ugk, uuk = ("ug", j % 2), ("uu", j % 2)
                for ahead in range(1, NWF):
                    wload(kglob + ahead)
                for half, pm, dst, dk in ((0, pmg, ug, ugk), (1, pmu, uu, uuk)):
                    P.add("act", CALL("activation", out=dst[:, 0:n], in_=psM[pm][:, 1:n + 1], func=AF.Identity,
                                      scale=cwc(1, half, j), bias=cwc(3, half, j)),
                          reads=[("pm", pm), CON], writes=[dk])
                    P.add("dve", CALL("scalar_tensor_tensor", out=dst[:, 0:n], in0=psM[pm][:, 0:n], scalar=cwc(0, half, j),
                                      in1=dst[:, 0:n], op0=ALU.mult, op1=ALU.add),
                          reads=[("pm", pm), dk, CON], writes=[dk])
                    P.add("dve", CALL("scalar_tensor_tensor", out=dst[:, 0:n], in0=psM[pm][:, 2:n + 2], scalar=cwc(2, half, j),
                                      in1=dst[:, 0:n], op0=ALU.mult, op1=ALU.add),
                          reads=[("pm", pm), dk, CON], writes=[dk])
                P.add("act", CALL("activation", out=ug[:, 0:n], in_=ug[:, 0:n], func=AF.Silu), reads=[ugk], writes=[ugk])
                P.add("pool", CALL("tensor_tensor", out=fT[:, j, 0:n], in0=ug[:, 0:n], in1=uu[:, 0:n], op=ALU.mult),
                      reads=[ugk, uuk], writes=[("fT", j)])
                yield
            h2own[hb] = None
            c_fin[0] = wi
            yield "out"
            fkeys = [("fT", j) for j in range(NPAIR)]
            for bi in range(nb):
                b = b0 + bi
                s = slot_of[b]
                X = xs[s]
                S = sc[b % NS]
                sk = lambda c, b=b: ("sc", b % NS, c)
                for nn in range(2):
                    pm = next_pm()
                    for j in range(NPAIR):
                        P.add("pe", CALL("matmul", psM[pm][:], lhsT=fT[:, j, bi * 128:(bi + 1) * 128],
                                         rhs=Wd[:, j, nn * 512:(nn + 1) * 512], start=(j == 0), stop=(j == NPAIR - 1)),
                              reads=fkeys + ["Wd"], writes=[("pm", pm)])
                    P.add("act", CALL("activation", out=junk[:, 0:512], in_=psM[pm][:], func=AF.Square,
                                      accum_out=S[:, 13 + nn:14 + nn]),
                          reads=[("pm", pm)], writes=[sk(13 + nn), "junk"])
                    ck = [("ug", 0), ("ug", 1)] if nn == 0 else [("uu", 0), ("uu", 1)]
                    P.add("dve", CALL("tensor_tensor", out=tmpC[:, nn * 512:(nn + 1) * 512], in0=psM[pm][:],
                                      in1=cp[:, C_G3 + nn * 512:C_G3 + (nn + 1) * 512], op=ALU.mult),
                          reads=[("pm", pm), CON], writes=ck)
                    yield "out"
                P.add("pool", CALL("tensor_tensor", out=S[:, 15:16], in0=S[:, 13:14], in1=S[:, 14:15], op=ALU.add),
                      reads=[sk(13), sk(14)], writes=[sk(15)])
                rsqrt_pool(S[:, 16:17], S[:, 15:16], 1.0 / D, [sk(15)], [sk(16)])
                for nn in range(2):
                    P.add("dve", CALL("scalar_tensor_tensor", out=X[:, nn * 512:(nn + 1) * 512],
                                      in0=tmpC[:, nn * 512:(nn + 1) * 512], scalar=S[:, 16:17],
                                      in1=X[:, nn * 512:(nn + 1) * 512], op0=ALU.mult, op1=ALU.add),
                          reads=([("ug", 0), ("ug", 1)] if nn == 0 else [("uu", 0), ("uu", 1)]) + [sk(16), ("xs", s)],
                          writes=[("xs", s)])
                P.add("pool", CALL("dma_start", out=y_d[b * 128:(b + 1) * 128, :], in_=X[:]),
                      reads=[("xs", s)], writes=[("y", b)], dma_key=f"ys{s}")
                owner[s] = None
                yield "out"

        b_done = [-1]
        c_fin = [-1]
        a_loaded = [-1]

        def ab_stream():
            BLAG = 3
            a_next, a_done, b_next = 0, -1, 0
            ga = None
            bq = []
            turn = 0
            while b_done[0] < NBLK - 1:
                if ga is None and a_next < NBLK and a_next - 3 <= b_done[0]:
                    if a_loaded[0] < a_next:
                        load_A(a_next)
                        a_loaded[0] = a_next
                    ga = gen_A(a_next)
                    if a_next + 1 < NBLK and a_loaded[0] < a_next + 1 and NXT >= 2:
                        load_A(a_next + 1)
                        a_loaded[0] = a_next + 1
                if len(bq) < 2 and b_next < NBLK and a_done >= min(b_next + 1, NBLK - 1):
                    wi_b, bi_b = win_of[b_next]
                    need_c = wi_b - 2 if bi_b == wins[wi_b][1] - 1 else wi_b - 3
                    ok = c_fin[0] >= need_c and (not bq or bq[0][2] >= 1 + BLAG)
                    if ok and (b_next in slot_of or alloc_B(b_next)):
                        bq.append([gen_B(b_next), b_next, 0])
                        b_next += 1
                cands = []
                if ga is not None:
                    cands.append("A")
                if bq:
                    cands.append("B0")
                if len(bq) == 2 and bq[0][2] >= bq[1][2] + 1 + BLAG:
                    cands.append("B1")
                if not cands:
                    yield "blocked"
                    continue
                kind = cands[turn % len(cands)]
                turn += 1
                if kind == "A":
                    try:
                        next(ga)
                        yield "step"
                    except StopIteration:
                        a_done = a_next
                        a_next += 1
                        ga = None
                else:
                    ent = bq[0] if kind == "B0" else bq[1]
                    try:
                        next(ent[0])
                        ent[2] += 1
                        yield "step"
                    except StopIteration:
                        assert kind == "B0"
                        b_done[0] = ent[1]
                        bq.pop(0)
                        if bq:
                            pass

        def ready(wi):
            b0, nb = wins[wi]
            last_needed = b0 + nb if wi < NWIN - 1 else b0 + nb - 1
            return b_done[0] >= last_needed

        ab = ab_stream()
        ab_fin = False
        cgen = None
        next_win = 0
        credit = 0.0
        while True:
            if cgen is None and next_win < NWIN and ready(next_win):
                cgen = gen_C(next_win)
                next_win += 1
            if cgen is not None:
                try:
                    tag = next(cgen)
                except StopIteration:
                    cgen = None
                    continue
                credit += R_OUT if tag == "out" else R_PAIR
                while credit >= 1.0 and not ab_fin:
                    t = next(ab, "done")
                    if t == "done":
                        ab_fin = True
                    elif t == "blocked":
                        credit = 0.0
                        break
                    else:
                        credit -= 1.0
            else:
                if ab_fin:
                    assert next_win == NWIN
                    break
                t = next(ab, "done")
                if t == "done":
                    ab_fin = True
                else:
                    assert t != "blocked", "scheduler deadlock: AB stream blocked with no active C window"
        P.add("sp", CALL("nop"), reads=[("y", b) for b in range(NBLK)])
        P.emit(nc, st)
    return nc


def _rope_tables(positions):
    inv_freq = (np.float32(500000.0) ** (-(np.arange(0, 16, 2, dtype=np.float32)) / np.float32(16))).astype(np.float32)
    ang = positions.astype(np.float32)[:, None] * inv_freq[None, :]
    return np.cos(ang).astype(np.float32), np.sin(ang).astype(np.float32)


def prepare_weights(inp):
    f = lambda a: np.asarray(a, dtype=np.float32)
    w_in = f(inp["w_in"])[0]
    perm = []
    for g in range(4):
        perm += list(range(g * 64, g * 64 + 64)) + list(range((4 + g) * 64, (4 + g) * 64 + 64))
    cols = np.concatenate([np.array(perm), np.arange(512, DIN)])
    w_in = w_in[:, cols]
    w_in_l = np.ascontiguousarray(w_in.reshape(8, 128, DIN).transpose(1, 0, 2))
    w_o_l = np.ascontiguousarray(f(inp["w_o"])[0].reshape(8, 128, D).transpose(1, 0, 2))
    wfi = f(inp["w_ffn_in"])[0]
    gate = wfi[:, :DFF].reshape(8, 128, NPAIR, 128)
    up = wfi[:, DFF:].reshape(8, 128, NPAIR, 128)
    w_fi_l = np.ascontiguousarray(np.concatenate([gate, up], axis=3).transpose(1, 0, 2, 3))
    w_d_l = np.ascontiguousarray(f(inp["w_ffn_out"])[0].reshape(NPAIR, 128, D).transpose(1, 0, 2))
    w_s_l = np.ascontiguousarray(f(inp["gmlp_w_s"])[0].transpose(2, 0, 1))
    cp = np.zeros((128, C_TOT), np.float32)
    cp[:, C_G1:C_G1 + 8] = f(inp["norm_mix_pre"])[0].reshape(8, 128).T
    gmix = np.concatenate([f(inp["out_norm_attn"])[0], f(inp["out_norm_gmlp"])[0]])
    cp[:, C_GMIX:C_GMIX + 8] = gmix.reshape(8, 128).T
    cp[:, C_G2:C_G2 + 8] = f(inp["norm_ffn_pre"])[0].reshape(8, 128).T
    cp[:, C_SINK:C_SINK + 8] = f(inp["attn_sink"])[0][None, :]
    cp[:, C_BS:C_BS + 8] = f(inp["gmlp_b_s"])[0].T
    cw = f(inp["conv_w"])[0]
    cbias = f(inp["conv_b"])[0]
    allc = np.concatenate([cw, cbias[None, :]], axis=0)
    a = allc.reshape(4, 2, NPAIR, 128).transpose(3, 0, 1, 2).reshape(128, 4 * 2 * NPAIR)
    cp[:, C_CW:C_CW + 176] = a
    cp[:, C_NH:C_NH + 8] = -0.5
    cp[:, C_LNG:C_LNG + 512] = f(inp["gmlp_ln_g"])[0][None, :]
    cp[:, C_LNB:C_LNB + 512] = f(inp["gmlp_ln_b"])[0][None, :]
    cp[:, C_GPOST:C_GPOST + D] = f(inp["norm_mix_post"])[0][None, :]
    cp[:, C_G3:C_G3 + D] = f(inp["norm_ffn_post"])[0][None, :]
    cb = np.zeros((128, 640), np.float32)
    cb[:, 0:128] = np.eye(128)
    jj = np.arange(128)[:, None]
    ii = np.arange(128)[None, :]
    visP = (jj >= ii)
    visN = (jj <= ii)
    cb[:, 128:256] = np.where(visP, 0.0, -30000.0)
    cb[:, 256:384] = np.where(visN, 0.0, -30000.0)
    cb[:, 384:512] = visP
    cb[:, 512:640] = visN
    cb = cb.astype(ml_dtypes.bfloat16)
    return dict(w_in=w_in_l, w_o=w_o_l, w_fi=w_fi_l, w_d=w_d_l, w_s=w_s_l), cp, cb


def core_maps(units_per_core, cont_flags, weights, cp, cb, UB):
    maps = []
    for units, cont in zip(units_per_core, cont_flags):
        x = np.ascontiguousarray(np.concatenate(units, axis=0).astype(np.float32))
        L = UB * 128
        if cont:
            pos = np.concatenate([np.arange(2 * L), np.arange(L)])
        else:
            pos = np.concatenate([np.arange(L)] * 3)
        c, s = _rope_tables(pos)
        rope = np.concatenate([c.reshape(3 * UB, 128, 8), s.reshape(3 * UB, 128, 8)], axis=2).transpose(1, 0, 2)
        cpc = cp.copy()
        cpc[:, C_FLAG] = 1.0 if cont else 0.0
        m = dict(weights)
        m.update(x=x, cpack=cpc, rope=np.ascontiguousarray(rope.astype(np.float32)), cbf=cb)
        maps.append(m)
    return maps


_NC_CACHE = {}


def kernel(**inputs):
    UB = 32
    xp = np.asarray(inputs["x_prompt"], dtype=np.float32)
    xsm = np.asarray(inputs["x_sample"], dtype=np.float32)
    weights, cp, cb = prepare_weights(inputs)
    units, flags = [], []
    for c in range(4):
        units.append([xp[c, :4096], xp[c, 4096:], xsm[c]])
        flags.append(True)
    for c in range(4):
        units.append([xsm[4 + 3 * c + i] for i in range(3)])
        flags.append(False)
    maps = core_maps(units, flags, weights, cp, cb, UB)
    if UB not in _NC_CACHE:
        _NC_CACHE[UB] = build_program(UB)
    nc = _NC_CACHE[UB]
    res = run_bass_kernel_spmd(nc, maps, core_ids=list(range(8)))
    yp = np.empty((4, 8192, D), np.float32)
    ysm = np.empty((16, 4096, D), np.float32)
    for c in range(4):
        y = res.results[c]["y"]
        yp[c] = y[:8192]
        ysm[c] = y[8192:]
    for c in range(4):
        y = res.results[4 + c]["y"]
        for i in range(3):
            ysm[4 + 3 * c + i] = y[i * 4096:(i + 1) * 4096]
    return (yp, ysm)
```

```python
import numpy as np
from contextlib import ExitStack
import ml_dtypes
import concourse.bass as bass
import concourse.mybir as mybir
from concourse.bass_utils import run_bass_kernel_spmd

F32 = mybir.dt.float32
BF16 = mybir.dt.bfloat16
AF = mybir.ActivationFunctionType
ALU = mybir.AluOpType

ENGS = ("pe", "act", "dve", "pool", "sp")
EPS = 1e-6
D = 1024
DFF = 2816
NPAIR = 22
DIN = 1792


class CALL:
    __slots__ = ("meth", "args", "kw")

    def __init__(self, meth, *args, **kw):
        self.meth = meth
        self.args = args
        self.kw = kw

    def __call__(self, engine):
        return getattr(engine, self.meth)(*self.args, **self.kw)


class Op:
    __slots__ = ("eng", "fn", "deps", "signal", "rank", "dma_key", "dma_val")

    def __init__(self, eng, fn, dma_key=None):
        self.eng = eng
        self.fn = fn
        self.deps = []
        self.signal = False
        self.rank = 0
        self.dma_key = dma_key
        self.dma_val = 0


class Prog:
    def __init__(self):
        self.ops = {e: [] for e in ENGS}
        self.lastw = {}
        self.readers = {}
        self.dma_counts = {}

    def add(self, eng, fn, reads=(), writes=(), dma_key=None, nodep=False):
        op = Op(eng, fn, dma_key)
        px = [k for k in reads if isinstance(k, tuple) and k[0] in ("pm", "pt")]
        if px:
            writes = list(writes) + px
        deps = {}
        if not nodep:
            for k in reads:
                w = self.lastw.get(k)
                if w is not None:
                    deps[id(w)] = w
            for k in writes:
                w = self.lastw.get(k)
                if w is not None:
                    deps[id(w)] = w
                for r in self.readers.get(k, {}).values():
                    deps[id(r)] = r
        for d in deps.values():
            if d.eng == "pe" and eng == "pe" and d.dma_key is None and dma_key is None:
                continue
            op.deps.append(d)
            d.signal = True
        rk = dma_key if dma_key is not None else eng
        for k in reads:
            self.readers.setdefault(k, {})[rk] = op
        for k in writes:
            self.lastw[k] = op
            self.readers[k] = {}
        if dma_key is not None:
            c = self.dma_counts.get(dma_key, 0) + 1
            self.dma_counts[dma_key] = c
            op.dma_val = 16 * c
        self.ops[eng].append(op)
        return op

    def emit(self, nc, stack):
        esem = {e: stack.enter_context(nc.semaphore("s_" + e)) for e in ENGS}
        dsem = {k: stack.enter_context(nc.semaphore("d_" + str(k))) for k in self.dma_counts}
        for e in ENGS:
            r = 0
            for op in self.ops[e]:
                if op.dma_key is None and op.signal:
                    r += 1
                    op.rank = r

        def run(eng_name):
            def body(engine):
                known = {}
                for op in self.ops[eng_name]:
                    for d in op.deps:
                        if d.dma_key is not None:
                            sem, val, key = dsem[d.dma_key], d.dma_val, ("d", d.dma_key)
                        else:
                            sem, val, key = esem[d.eng], d.rank, ("e", d.eng)
                        if known.get(key, 0) < val:
                            engine.wait_ge(sem, val)
                            known[key] = val
                    ins = op.fn(engine)
                    if op.dma_key is not None:
                        ins.then_inc(dsem[op.dma_key], 16)
                    elif op.signal:
                        ins.then_inc(esem[eng_name], 1)
            return body

        with nc.Block() as block:
            block.tensor(run("pe"))
            block.scalar(run("act"))
            block.vector(run("dve"))
            block.gpsimd(run("pool"))
            block.sync(run("sp"))


C_G1, C_GMIX, C_G2, C_SINK, C_BS = 0, 8, 16, 24, 32
C_CW = 40
C_FLAG = 216
C_NH = 217
C_LNG = 232
C_LNB = C_LNG + 512
C_GPOST = C_LNB + 512
C_G3 = C_GPOST + 1024
C_TOT = C_G3 + 1024


def window_sizes(UB, WB=2):
    w = []
    r = UB
    while r > 0:
        s = min(WB, r)
        w.append(s)
        r -= s
    return w


def build_program(UB, NX=5, NXT=2, NWF=2, WB=2, R_PAIR=1.0, R_OUT=1.5):
    NU = 3
    NBLK = NU * UB
    wsz = window_sizes(UB, WB)
    wins = []
    for u in range(NU):
        b0 = u * UB
        for s in wsz:
            wins.append((b0, s))
            b0 += s
    win_of = {}
    for wi, (b0, s) in enumerate(wins):
        for i in range(s):
            win_of[b0 + i] = (wi, i)
    NWIN = len(wins)
    HW = WB * 128 + 4

    nc = bass.Bass("TRN2", target_bir_lowering=False)
    dt_in = lambda name, shape, dt: nc.dram_tensor(name, list(shape), dt, kind="ExternalInput").ap()
    x_d = dt_in("x", [NBLK * 128, D], F32)
    win_d = dt_in("w_in", [128, 8, DIN], F32)
    wo_d = dt_in("w_o", [128, 8, D], F32)
    wfi_d = dt_in("w_fi", [128, 8, NPAIR, 256], F32)
    wd_d = dt_in("w_d", [128, NPAIR, D], F32)
    ws_d = dt_in("w_s", [128, 8, 128], F32)
    cp_d = dt_in("cpack", [128, C_TOT], F32)
    rope_d = dt_in("rope", [128, NBLK, 16], F32)
    cb_d = dt_in("cbf", [128, 640], BF16)
    y_d = nc.dram_tensor("y", [NBLK * 128, D], F32, kind="ExternalOutput").ap()
    scr = nc.dram_tensor("wfi_scr", [NPAIR, 128, 8, 256], BF16, kind="Internal").ap()

    P = Prog()
    with ExitStack() as st:
        def sb(name, shape, dt):
            return st.enter_context(nc.sbuf_tensor(name, list(shape), dt))

        cp = sb("cp", [128, C_TOT], F32)
        ropeR = [sb(f"rope{i}", [128, 16], F32) for i in range(4)]
        cb = sb("cb", [128, 640], BF16)
        mskc = sb("mskc", [128, 256], BF16)
        esink = sb("esink", [128, 8], F32)
        Win = sb("Win", [128, 8, DIN], BF16)
        Wo = sb("Wo", [128, 8, D], BF16)
        Wd = sb("Wd", [128, NPAIR, D], BF16)
        WsT = sb("WsT", [128, 8, 128], BF16)
        wfi = [sb(f"wfi{i}", [128, 8, 256], BF16) for i in range(NWF)]
        xs = [sb(f"xs{i}", [128, D], F32) for i in range(NX)]
        xa = [sb(f"xa{i}", [128, D], F32) for i in range(NXT)]
        xbf = sb("xbf", [128, D], BF16)
        junk = sb("junk", [128, D], BF16)
        xT = sb("xT", [128, 8, 128], BF16)
        qkbf = sb("qkbf", [128, 10, 64], BF16)
        r32 = sb("r32", [128, 10, 16], F32)
        rt = sb("rt", [128, 4, 10, 8], F32)
        NQ, NK, NG = 3, 4, 3
        QT = [sb(f"QT{i}", [128, 4, 128], BF16) for i in range(NQ)]
        KT = [sb(f"KT{i}", [128, 128], BF16) for i in range(NK)]
        Vx = [sb(f"Vx{i}", [128, 2, 66], BF16) for i in range(NK)]
        gu = sb("gu", [128, 512], F32)
        gv = sb("gv", [128, 512], F32)
        vln = sb("vln", [128, 512], BF16)
        mixg = [sb(f"mixg{i}", [128, 512], BF16) for i in range(NG)]
        st6 = sb("st6", [128, 6], F32)
        mv = sb("mv", [128, 2], F32)
        NS = 4
        sc = [sb(f"sc{i}", [128, 24], F32) for i in range(NS)]
        PT = [sb(f"PT{i}", [128, 512], BF16) for i in range(6)]
        a32 = sb("a32", [128, 512], F32)
        den = sb("den", [128, 8], F32)
        mixa = sb("mixa", [128, 512], BF16)
        mixT = sb("mixT", [128, 8, 128], BF16)
        tmpB = sb("tmpX", [128, D], F32)
        h2bf = sb("h2bf", [128, D], BF16)
        NH = 3
        h2T = [sb(f"h2T{i}", [128, 8, HW], BF16) for i in range(NH)]
        fT = sb("fT", [128, NPAIR, WB * 128], BF16)
        cbuf = sb("cbuf", [128, 4, 256], F32)
        ugs = [cbuf[:, i, :] for i in range(2)]
        uus = [cbuf[:, 2 + i, :] for i in range(2)]
        tmpC = cbuf[:].rearrange("p a c -> p (a c)")
        psT = [st.enter_context(nc.psum_tensor(f"psT{i}", [128, 1024], BF16)) for i in range(2)]
        NPM = 6
        psM = [st.enter_context(nc.psum_tensor(f"psM{i}", [128, 512], F32)) for i in range(NPM)]

        ident = cb[:, 0:128]
        maskP = cb[:, 128:256]
        maskN = cb[:, 256:384]
        maskPc = mskc[:, 0:128]
        maskNc = mskc[:, 128:256]
        flag = cp[:, C_FLAG:C_FLAG + 1]
        nhalf = cp[:, C_NH:C_NH + 1]

        cnt = {"pm": 0, "pt": 0, "wf": 0}

        def next_pm():
            i = cnt["pm"] % NPM
            cnt["pm"] += 1
            return i

        def next_pt():
            i = cnt["pt"] % 2
            cnt["pt"] += 1
            return i

        CON = "consts"

        P.add("sp", CALL("dma_start", out=cp[:], in_=cp_d), writes=[CON], dma_key="c0")
        P.add("sp", CALL("dma_start", out=cb[:], in_=cb_d), writes=["cb"], dma_key="c2")
        P.add("pool", CALL("dma_start", out=Wd[:], in_=wd_d), writes=["Wd"], dma_key="c3")
        P.add("pool", CALL("dma_start", out=WsT[:], in_=ws_d), writes=["WsT"], dma_key="c4")
        P.add("act", CALL("activation", out=esink[:], in_=cp[:, C_SINK:C_SINK + 8], func=AF.Exp),
              reads=[CON], writes=["esink"])
        for (dst_m, src_m) in ((maskPc, cb[:, 384:512]), (maskNc, cb[:, 512:640])):
            P.add("dve", CALL("tensor_scalar", out=a32[:, 0:128], in0=src_m, scalar1=flag, scalar2=-1.0, op0=ALU.mult, op1=ALU.add),
                  reads=[CON, "cb"], writes=[("a32", 0)])
            P.add("dve", CALL("tensor_scalar", out=dst_m, in0=a32[:, 0:128], scalar1=30000.0, scalar2=None, op0=ALU.mult),
                  reads=[("a32", 0)], writes=["mskc"])
        for i in range(NK):
            P.add("dve", CALL("memset", Vx[i][:, :, 64:66], 1.0), writes=[("Vx", i)])
        for i in range(NH):
            P.add("dve", CALL("memset", h2T[i][:], 0.0), writes=[("h2T", i, "L"), ("h2T", i, "R")])

        fTflat = fT[:].rearrange("p j t -> p (j t)")
        piece = {"i": 0}

        def stage_piece(src_ap, ncols, gcol, dst_ap, dst_keys, post=None):
            i = piece["i"]
            piece["i"] += 1
            s = i % NX
            P.add("sp", CALL("dma_start", out=xs[s][:, 0:ncols], in_=src_ap), writes=[("xs", s)], dma_key=f"xs{s}")
            P.add("dve", CALL("tensor_scalar", out=dst_ap, in0=xs[s][:, 0:ncols], scalar1=cp[:, gcol:gcol + 1],
                              scalar2=None, op0=ALU.mult),
                  reads=[("xs", s), CON], writes=dst_keys)
            if post is not None:
                post()

        for k in range(8):
            stage_piece(win_d[:, k, 0:1024], 1024, C_G1 + k, Win[:, k, 0:1024], ["Win"])
            stage_piece(win_d[:, k, 1024:DIN], DIN - 1024, C_G1 + k, Win[:, k, 1024:DIN], ["Win"])
            stage_piece(wo_d[:, k, :], 1024, C_GMIX + k, Wo[:, k, :], ["Wo"])
        NSTG = 3
        sidx = 0
        for k in range(8):
            for jg in range(6):
                j0 = jg * 4
                nj = min(4, NPAIR - j0)
                ncols = nj * 256
                sl = sidx % NSTG
                sidx += 1
                stg = fTflat[:, sl * 1024: sl * 1024 + ncols]
                src = wfi_d[:, k, j0:j0 + nj, :].rearrange("p j c -> p (j c)")
                dst = scr[j0:j0 + nj, :, k, :].rearrange("j p c -> p j c")

                def post(stg=stg, dst=dst, sl=sl, jg=jg, k=k, nj=nj):
                    P.add("sp", CALL("dma_start", out=dst, in_=stg.rearrange("p (j c) -> p j c", j=nj)),
                          reads=[("stg", sl)], writes=[("scr", jg, k)], dma_key=f"stg{sl}")
                stage_piece(src, ncols, C_G2 + k, stg, [("stg", sl)], post)
        SCR_KEYS = [("scr", jg, k) for jg in range(6) for k in range(8)]

        def rsqrt_pool(out_ap, in_ap, scale, keys_r, keys_w):
            P.add("pool", CALL("tensor_scalar", out=out_ap, in0=in_ap, scalar1=scale, scalar2=EPS,
                               op0=ALU.mult, op1=ALU.add), reads=keys_r, writes=keys_w)
            P.add("pool", CALL("tensor_tensor", out=out_ap, in0=out_ap, in1=nhalf, op=ALU.pow),
                  reads=keys_w + [CON], writes=keys_w)

        owner = [None] * NX
        slot_of = {}

        def load_A(b):
            t = b % NXT
            P.add("sp", CALL("dma_start", out=xa[t][:], in_=x_d[b * 128:(b + 1) * 128, :]),
                  writes=[("xa", t)], dma_key=f"xa{t}")
            P.add("sp", CALL("dma_start", out=ropeR[b % 4][:], in_=rope_d[:, b, :]),
                  writes=[("rope", b % 4)], dma_key=f"rp{b % 4}")

        def alloc_B(b):
            s = None
            for i in range(NX):
                if owner[i] is None:
                    s = i
                    break
            if s is None:
                return False
            owner[s] = b
            slot_of[b] = s
            return True

        def reload_x(b):
            s = slot_of[b]
            P.add("sp", CALL("dma_start", out=xs[s][:], in_=x_d[b * 128:(b + 1) * 128, :]),
                  writes=[("xs", s)], dma_key=f"xs{s}")

        def unit_pos(b):
            return b // UB, b % UB

        def gen_A(b):
            X = xa[b % NXT]
            xk = ("xa", b % NXT)
            S = sc[b % NS]
            sk = lambda c: ("sc", b % NS, c)
            P.add("act", CALL("activation", out=junk[:], in_=X[:], func=AF.Square, accum_out=S[:, 0:1]),
                  reads=[xk], writes=[sk(0), "junk"])
            rsqrt_pool(S[:, 1:2], S[:, 0:1], 1.0 / D, [sk(0)], [sk(1)])
            rstd1 = S[:, 1:2]
            P.add("act", CALL("activation", out=xbf[:], in_=X[:], func=AF.Copy), reads=[xk], writes=["xbf"])
            yield
            pt = next_pt()
            for k in range(8):
                P.add("pe", CALL("transpose", out=psT[pt][:, k * 128:(k + 1) * 128],
                                 in_=xbf[:, k * 128:(k + 1) * 128], identity=ident),
                      reads=["xbf", "cb"], writes=[("pt", pt)])
            P.add("act", CALL("copy", out=xT[:].rearrange("p k t -> p (k t)"), in_=psT[pt][:]),
                  reads=[("pt", pt)], writes=["xT"])
            yield
            jobs = [(0, 512), (512, 256), (768, 512), (1280, 512)]
            pms = []
            for (c0, n) in jobs:
                pm = next_pm()
                pms.append(pm)
                for k in range(8):
                    P.add("pe", CALL("matmul", psM[pm][:, 0:n], lhsT=xT[:, k, :], rhs=Win[:, k, c0:c0 + n],
                                     start=(k == 0), stop=(k == 7)),
                          reads=["xT", "Win"], writes=[("pm", pm)])
            pq, pkv, pu, pv = pms
            P.add("act", CALL("activation", out=qkbf[:, 0:8, :].rearrange("p h d -> p (h d)"), in_=psM[pq][:, 0:512],
                              func=AF.Identity, scale=rstd1),
                  reads=[("pm", pq), sk(1)], writes=["qkbf"])
            P.add("dve", CALL("tensor_scalar", out=r32[:, 0:8, :],
                              in0=psM[pq][:, 0:512].rearrange("p (h d) -> p h d", d=64)[:, :, 0:16],
                              scalar1=rstd1, scalar2=None, op0=ALU.mult),
                  reads=[("pm", pq), sk(1)], writes=["r32"])
            P.add("act", CALL("activation", out=qkbf[:, 8:10, :].rearrange("p h d -> p (h d)"), in_=psM[pkv][:, 0:128],
                              func=AF.Identity, scale=rstd1),
                  reads=[("pm", pkv), sk(1)], writes=["qkbf"])
            vs = b % NK
            P.add("dve", CALL("tensor_scalar", out=Vx[vs][:, :, 0:64],
                              in0=psM[pkv][:, 128:256].rearrange("p (a d) -> p a d", d=64),
                              scalar1=rstd1, scalar2=None, op0=ALU.mult),
                  reads=[("pm", pkv), sk(1)], writes=[("Vx", vs)])
            P.add("dve", CALL("tensor_scalar", out=r32[:, 8:10, :],
                              in0=psM[pkv][:, 0:128].rearrange("p (h d) -> p h d", d=64)[:, :, 0:16],
                              scalar1=rstd1, scalar2=None, op0=ALU.mult),
                  reads=[("pm", pkv), sk(1)], writes=["r32"])
            P.add("act", CALL("activation", out=gu[:], in_=psM[pu][:], func=AF.Gelu, scale=rstd1),
                  reads=[("pm", pu), sk(1)], writes=["gu"])
            P.add("act", CALL("activation", out=gv[:], in_=psM[pv][:], func=AF.Gelu, scale=rstd1),
                  reads=[("pm", pv), sk(1)], writes=["gv"])
            P.add("dve", CALL("bn_stats", out=st6[:], in_=gv[:]), reads=["gv"], writes=["st6"])
            P.add("dve", CALL("bn_aggr", out=mv[:], in_=st6[:]), reads=["st6"], writes=["mv"])
            rsqrt_pool(S[:, 2:3], mv[:, 1:2], 1.0, ["mv"], [sk(2)])
            P.add("dve", CALL("scalar_tensor_tensor", out=gv[:], in0=gv[:], scalar=mv[:, 0:1],
                              in1=cp[:, C_LNG:C_LNG + 512], op0=ALU.subtract, op1=ALU.mult),
                  reads=["gv", "mv", CON], writes=["gv"])
            P.add("dve", CALL("scalar_tensor_tensor", out=vln[:], in0=gv[:], scalar=S[:, 2:3],
                              in1=cp[:, C_LNB:C_LNB + 512], op0=ALU.mult, op1=ALU.add),
                  reads=["gv", sk(2), CON], writes=["vln"])
            rk = ("rope", b % 4)
            cosb = ropeR[b % 4][:, 0:8].unsqueeze(1).broadcast_to([128, 10, 8])
            sinb = ropeR[b % 4][:, 8:16].unsqueeze(1).broadcast_to([128, 10, 8])
            x1 = r32[:, :, 0:8]
            x2 = r32[:, :, 8:16]
            P.add("dve", CALL("tensor_tensor", out=rt[:, 0], in0=x1, in1=cosb, op=ALU.mult), reads=["r32", rk], writes=["rt0"])
            P.add("dve", CALL("tensor_tensor", out=rt[:, 1], in0=x2, in1=sinb, op=ALU.mult), reads=["r32", rk], writes=["rt1"])
            P.add("dve", CALL("tensor_tensor", out=rt[:, 2], in0=x2, in1=cosb, op=ALU.mult), reads=["r32", rk], writes=["rt2"])
            P.add("dve", CALL("tensor_tensor", out=rt[:, 3], in0=x1, in1=sinb, op=ALU.mult), reads=["r32", rk], writes=["rt3"])
            P.add("dve", CALL("tensor_tensor", out=qkbf[:, :, 0:8], in0=rt[:, 0], in1=rt[:, 1], op=ALU.subtract),
                  reads=["rt0", "rt1"], writes=["qkbf"])
            P.add("dve", CALL("tensor_tensor", out=qkbf[:, :, 8:16], in0=rt[:, 2], in1=rt[:, 3], op=ALU.add),
                  reads=["rt2", "rt3"], writes=["qkbf"])
            yield
            yield
            pg = next_pm()
            for g in range(8):
                P.add("pe", CALL("matmul", psM[pg][:, g * 64:(g + 1) * 64], lhsT=WsT[:, g, :],
                                 rhs=vln[:, g * 64:(g + 1) * 64], start=True, stop=True),
                      reads=["vln", "WsT"], writes=[("pm", pg)])
            bsb = cp[:, C_BS:C_BS + 8].unsqueeze(2).broadcast_to([128, 8, 64])
            P.add("dve", CALL("tensor_tensor", out=gv[:].rearrange("p (g d) -> p g d", d=64),
                              in0=psM[pg][:].rearrange("p (g d) -> p g d", d=64), in1=bsb, op=ALU.add),
                  reads=[("pm", pg), CON], writes=["gv"])
            P.add("pool", CALL("tensor_tensor", out=gv[:], in0=gv[:], in1=gu[:], op=ALU.mult),
                  reads=["gv", "gu"], writes=["gv"])
            P.add("act", CALL("activation", out=junk[:, 0:512], in_=gv[:], func=AF.Square, accum_out=S[:, 3:4]),
                  reads=["gv"], writes=[sk(3), "junk"])
            rsqrt_pool(S[:, 4:5], S[:, 3:4], 1.0 / 512, [sk(3)], [sk(4)])
            ms = b % NG
            P.add("pool", CALL("tensor_scalar", out=mixg[ms][:], in0=gv[:], scalar1=S[:, 4:5], scalar2=1.0,
                               op0=ALU.mult, op1=ALU.mult),
                  reads=["gv", sk(4)], writes=[("mixg", ms)])
            pt2 = next_pt()
            qkflat = qkbf[:].rearrange("p h d -> p (h d)")
            for c in range(5):
                P.add("pe", CALL("transpose", out=psT[pt2][:, c * 128:(c + 1) * 128],
                                 in_=qkflat[:, c * 128:(c + 1) * 128], identity=ident),
                      reads=["qkbf", "cb"], writes=[("pt", pt2)])
            qs = b % NQ
            P.add("act", CALL("copy", out=QT[qs][:].rearrange("p c t -> p (c t)"), in_=psT[pt2][:, 0:512]),
                  reads=[("pt", pt2)], writes=[("QT", qs)])
            P.add("act", CALL("copy", out=KT[vs][:], in_=psT[pt2][:, 512:640]),
                  reads=[("pt", pt2)], writes=[("KT", vs)])
            yield

        h2own = [None] * NH

        def h2claim(buf, wi):
            assert h2own[buf] in (None, wi), f"h2T buffer {buf} owned by window {h2own[buf]}, wanted by {wi}"
            h2own[buf] = wi

        def gen_B(b):
            u, pb = unit_pos(b)
            s = slot_of[b]
            X = xs[s]
            S = sc[b % NS]
            sk = lambda c: ("sc", b % NS, c)
            qs = b % NQ
            kbs = []
            if pb > 0:
                kbs.append((b - 1, maskP))
            elif u == 1:
                kbs.append((b - 1, maskPc))
            kbs.append((b, None))
            if pb < UB - 1:
                kbs.append((b + 1, maskN))
            elif u == 0:
                kbs.append((b + 1, maskNc))
            ptsl = {}
            for kvh in range(2):
                lo, hi = 64 * kvh, 64 * kvh + 64
                pts = []
                for i, (kb, msk) in enumerate(kbs):
                    pm = next_pm()
                    ks = kb % NK
                    P.add("pe", CALL("matmul", psM[pm][:], lhsT=KT[ks][lo:hi, :], rhs=QT[qs][lo:hi, :, :],
                                     start=True, stop=(msk is None)),
                          reads=[("KT", ks), ("QT", qs)], writes=[("pm", pm)])
                    if msk is not None:
                        P.add("pe", CALL("matmul", psM[pm][:].rearrange("p (g q) -> p g q", g=4), lhsT=ident,
                                         rhs=msk.unsqueeze(1).broadcast_to([128, 4, 128]), start=False, stop=True),
                              reads=["cb", "mskc"], writes=[("pm", pm)])
                    pi = kvh * 3 + i
                    pts.append(pi)
                    P.add("act", CALL("activation", out=PT[pi][:], in_=psM[pm][:], func=AF.Exp, scale=0.125),
                          reads=[("pm", pm)], writes=[("PT", pi)])
                ptsl[kvh] = pts
                yield
            for kvh in range(2):
                pts = ptsl[kvh]
                if kvh == 1:
                    reload_x(b)
                po = next_pm()
                O = psM[po][:, 0:260].rearrange("p (g d) -> p g d", d=65)
                for g in range(4):
                    for i, (kb, msk) in enumerate(kbs):
                        ks = kb % NK
                        pi = pts[i]
                        P.add("pe", CALL("matmul", O[:, g, :], lhsT=PT[pi][:, g * 128:(g + 1) * 128],
                                         rhs=Vx[ks][:, kvh, 0:65], start=(i == 0), stop=(i == len(kbs) - 1)),
                              reads=[("PT", pi), ("Vx", ks)], writes=[("pm", po)])
                dk = den[:, kvh * 4:(kvh + 1) * 4]
                P.add("dve", CALL("tensor_tensor", out=dk.unsqueeze(2), in0=O[:, :, 64:65],
                                  in1=esink[:, kvh * 4:(kvh + 1) * 4].unsqueeze(2), op=ALU.add),
                      reads=[("pm", po), "esink"], writes=[("den", kvh)])
                P.add("dve", CALL("reciprocal", out=dk, in_=dk), reads=[("den", kvh)], writes=[("den", kvh)])
                P.add("dve", CALL("tensor_tensor",
                                  out=a32[:, kvh * 256:(kvh + 1) * 256].rearrange("p (g d) -> p g d", d=64),
                                  in0=O[:, :, 0:64], in1=dk.unsqueeze(2).broadcast_to([128, 4, 64]), op=ALU.mult),
                      reads=[("pm", po), ("den", kvh)], writes=[("a32", kvh)])
                if kvh == 1:
                    P.add("act", CALL("activation", out=junk[:, 0:512], in_=a32[:], func=AF.Square, accum_out=S[:, 5:6]),
                          reads=[("a32", 0), ("a32", 1)], writes=[sk(5), "junk"])
                    rsqrt_pool(S[:, 6:7], S[:, 5:6], 1.0 / 512, [sk(5)], [sk(6)])
                    P.add("pool", CALL("tensor_scalar", out=mixa[:], in0=a32[:], scalar1=S[:, 6:7], scalar2=1.0,
                                       op0=ALU.mult, op1=ALU.mult),
                          reads=[("a32", 0), ("a32", 1), sk(6)], writes=["mixa"])
                    yield
                yield
            ms = b % NG
            ptm = next_pt()
            for c in range(4):
                P.add("pe", CALL("transpose", out=psT[ptm][:, c * 128:(c + 1) * 128],
                                 in_=mixa[:, c * 128:(c + 1) * 128], identity=ident),
                      reads=["mixa", "cb"], writes=[("pt", ptm)])
            for c in range(4):
                P.add("pe", CALL("transpose", out=psT[ptm][:, (4 + c) * 128:(5 + c) * 128],
                                 in_=mixg[ms][:, c * 128:(c + 1) * 128], identity=ident),
                      reads=[("mixg", ms), "cb"], writes=[("pt", ptm)])
            P.add("act", CALL("copy", out=mixT[:].rearrange("p k t -> p (k t)"), in_=psT[ptm][:]),
                  reads=[("pt", ptm)], writes=["mixT"])
            yield
            pmm = []
            for n in range(2):
                pm = next_pm()
                pmm.append(pm)
                for k in range(8):
                    P.add("pe", CALL("matmul", psM[pm][:], lhsT=mixT[:, k, :], rhs=Wo[:, k, n * 512:(n + 1) * 512],
                                     start=(k == 0), stop=(k == 7)),
                          reads=["mixT", "Wo"], writes=[("pm", pm)])
                P.add("act", CALL("activation", out=junk[:, 0:512], in_=psM[pm][:], func=AF.Square,
                                  accum_out=S[:, 7 + n:8 + n]),
                      reads=[("pm", pm)], writes=[sk(7 + n), "junk"])
            P.add("pool", CALL("tensor_tensor", out=S[:, 9:10], in0=S[:, 7:8], in1=S[:, 8:9], op=ALU.add),
                  reads=[sk(7), sk(8)], writes=[sk(9)])
            rsqrt_pool(S[:, 10:11], S[:, 9:10], 1.0 / D, [sk(9)], [sk(10)])
            for n in range(2):
                pm = pmm[n]
                P.add("dve", CALL("tensor_tensor", out=tmpB[:, n * 512:(n + 1) * 512], in0=psM[pm][:],
                                  in1=cp[:, C_GPOST + n * 512:C_GPOST + (n + 1) * 512], op=ALU.mult),
                      reads=[("pm", pm), CON], writes=[("tmpX", n)])
            for n in range(2):
                P.add("dve", CALL("scalar_tensor_tensor", out=X[:, n * 512:(n + 1) * 512], in0=tmpB[:, n * 512:(n + 1) * 512],
                                  scalar=S[:, 10:11], in1=X[:, n * 512:(n + 1) * 512], op0=ALU.mult, op1=ALU.add),
                      reads=[("tmpX", n), sk(10), ("xs", s)], writes=[("xs", s)])
            P.add("act", CALL("activation", out=junk[:], in_=X[:], func=AF.Square, accum_out=S[:, 11:12]),
                  reads=[("xs", s)], writes=[sk(11), "junk"])
            rsqrt_pool(S[:, 12:13], S[:, 11:12], 1.0 / D, [sk(11)], [sk(12)])
            P.add("act", CALL("activation", out=h2bf[:], in_=X[:], func=AF.Identity, scale=S[:, 12:13]),
                  reads=[("xs", s), sk(12)], writes=["h2bf"])
            yield
            yield
            yield
            pth = next_pt()
            for k in range(8):
                P.add("pe", CALL("transpose", out=psT[pth][:, k * 128:(k + 1) * 128],
                                 in_=h2bf[:, k * 128:(k + 1) * 128], identity=ident),
                      reads=["h2bf", "cb"], writes=[("pt", pth)])
            wi, bi = win_of[b]
            nb = wins[wi][1]
            hb = wi % NH
            h2claim(hb, wi)
            src = psT[pth][:].rearrange("p (k t) -> p k t", t=128)
            P.add("act", CALL("copy", out=h2T[hb][:, :, 2 + bi * 128:2 + (bi + 1) * 128], in_=src),
                  reads=[("pt", pth)], writes=[("h2T", hb, bi)])
            if bi == 0 and wi > 0:
                ph = (wi - 1) % NH
                h2claim(ph, wi - 1)
                rc = 2 + wins[wi - 1][1] * 128
                if pb > 0:
                    P.add("dve", CALL("tensor_copy", out=h2T[ph][:, :, rc:rc + 1], in_=src[:, :, 0:1]),
                          reads=[("pt", pth)], writes=[("h2T", ph, "R")])
                elif u == 1:
                    P.add("dve", CALL("tensor_scalar", out=h2T[ph][:, :, rc:rc + 1], in0=src[:, :, 0:1], scalar1=flag,
                                      scalar2=None, op0=ALU.mult),
                          reads=[("pt", pth), CON], writes=[("h2T", ph, "R")])
                else:
                    P.add("dve", CALL("memset", h2T[ph][:, :, rc:rc + 1], 0.0), writes=[("h2T", ph, "R")])
            if bi == nb - 1 and wi < NWIN - 1:
                nh = (wi + 1) % NH
                h2claim(nh, wi + 1)
                if pb < UB - 1:
                    P.add("dve", CALL("tensor_copy", out=h2T[nh][:, :, 1:2], in_=src[:, :, 127:128]),
                          reads=[("pt", pth)], writes=[("h2T", nh, "L")])
                elif u == 0:
                    P.add("dve", CALL("tensor_scalar", out=h2T[nh][:, :, 1:2], in0=src[:, :, 127:128], scalar1=flag,
                                      scalar2=None, op0=ALU.mult),
                          reads=[("pt", pth), CON], writes=[("h2T", nh, "L")])
                else:
                    P.add("dve", CALL("memset", h2T[nh][:, :, 1:2], 0.0), writes=[("h2T", nh, "L")])
            if wi == NWIN - 1 and bi == nb - 1:
                rc = 2 + nb * 128
                P.add("dve", CALL("memset", h2T[hb][:, :, rc:rc + 1], 0.0), writes=[("h2T", hb, "R")])
            yield

        def cwc(which, half, j):
            c = C_CW + (which * 2 + half) * NPAIR + j
            return cp[:, c:c + 1]

        wl = {"n": 0}

        def wload(k):
            while wl["n"] <= k and wl["n"] < NWIN * NPAIR:
                kk = wl["n"]
                wl["n"] += 1
                P.add("sp", CALL("dma_start", out=wfi[kk % NWF][:], in_=scr[kk % NPAIR]),
                      reads=SCR_KEYS, writes=[("wfi", kk % NWF)], dma_key=f"wf{kk % NWF}")

        def gen_C(wi):
            b0, nb = wins[wi]
            hb = wi % NH
            assert h2own[hb] == wi
            n = nb * 128
            N = n + 2
            hkeys = [("h2T", hb, i) for i in range(nb)] + [("h2T", hb, "L"), ("h2T", hb, "R")]
            for j in range(NPAIR):
                kglob = wi * NPAIR + j
                wload(kglob)
                wfs = kglob % NWF
                pmg = next_pm()
                pmu = next_pm()
                for half, pm in ((0, pmg), (1, pmu)):
                    for k in range(8):
                        P.add("pe", CALL("matmul", psM[pm][:, 0:N], lhsT=wfi[wfs][:, k, half * 128:(half + 1) * 128],
                                         rhs=h2T[hb][:, k, 1:N + 1], start=(k == 0), stop=(k == 7)),
                              reads=[("wfi", wfs)] + hkeys, writes=[("pm", pm)])
                ug, uu = ugs[j % 2], uus[j % 2]
                ugk, uuk = ("ug", j % 2), ("uu", j % 2)
                for ahead in range(1, NWF):
                    wload(kglob + ahead)
                for half, pm, dst, dk in ((0, pmg, ug, ugk), (1, pmu, uu, uuk)):
                    P.add("act", CALL("activation", out=dst[:, 0:n], in_=psM[pm][:, 1:n + 1], func=AF.Identity,
                                      scale=cwc(1, half, j), bias=cwc(3, half, j)),
                          reads=[("pm", pm), CON], writes=[dk])
                    P.add("dve", CALL("scalar_tensor_tensor", out=dst[:, 0:n], in0=psM[pm][:, 0:n], scalar=cwc(0, half, j),
                                      in1=dst[:, 0:n], op0=ALU.mult, op1=ALU.add),
                          reads=[("pm", pm), dk, CON], writes=[dk])
                    P.add("dve", CALL("scalar_tensor_tensor", out=dst[:, 0:n], in0=psM[pm][:, 2:n + 2], scalar=cwc(2, half, j),
                                      in1=dst[:, 0:n], op0=ALU.mult, op1=ALU.add),
                          reads=[("pm", pm), dk, CON], writes=[dk])
                P.add("act", CALL("activation", out=ug[:, 0:n], in_=ug[:, 0:n], func=AF.Silu), reads=[ugk], writes=[ugk])
                P.add("pool", CALL("tensor_tensor", out=fT[:, j, 0:n], in0=ug[:, 0:n], in1=uu[:, 0:n], op=ALU.mult),
                      reads=[ugk, uuk], writes=[("fT", j)])
                yield
            h2own[hb] = None
            c_fin[0] = wi
            yield "out"
            fkeys = [("fT", j) for j in range(NPAIR)]
            for bi in range(nb):
                b = b0 + bi
                s = slot_of[b]
                X = xs[s]
                S = sc[b % NS]
                sk = lambda c, b=b: ("sc", b % NS, c)
                for nn in range(2):
                    pm = next_pm()
                    for j in range(NPAIR):
                        P.add("pe", CALL("matmul", psM[pm][:], lhsT=fT[:, j, bi * 128:(bi + 1) * 128],
                                         rhs=Wd[:, j, nn * 512:(nn + 1) * 512], start=(j == 0), stop=(j == NPAIR - 1)),
                              reads=fkeys + ["Wd"], writes=[("pm", pm)])
                    P.add("act", CALL("activation", out=junk[:, 0:512], in_=psM[pm][:], func=AF.Square,
                                      accum_out=S[:, 13 + nn:14 + nn]),
                          reads=[("pm", pm)], writes=[sk(13 + nn), "junk"])
                    ck = [("ug", 0), ("ug", 1)] if nn == 0 else [("uu", 0), ("uu", 1)]
                    P.add("dve", CALL("tensor_tensor", out=tmpC[:, nn * 512:(nn + 1) * 512], in0=psM[pm][:],
                                      in1=cp[:, C_G3 + nn * 512:C_G3 + (nn + 1) * 512], op=ALU.mult),
                          reads=[("pm", pm), CON], writes=ck)
                    yield "out"
                P.add("pool", CALL("tensor_tensor", out=S[:, 15:16], in0=S[:, 13:14], in1=S[:, 14:15], op=ALU.add),
                      reads=[sk(13), sk(14)], writes=[sk(15)])
                rsqrt_pool(S[:, 16:17], S[:, 15:16], 1.0 / D, [sk(15)], [sk(16)])
                for nn in range(2):
                    P.add("dve", CALL("scalar_tensor_tensor", out=X[:, nn * 512:(nn + 1) * 512],
                                      in0=tmpC[:, nn * 512:(nn + 1) * 512], scalar=S[:, 16:17],
                                      in1=X[:, nn * 512:(nn + 1) * 512], op0=ALU.mult, op1=ALU.add),
                          reads=([("ug", 0), ("ug", 1)] if nn == 0 else [("uu", 0), ("uu", 1)]) + [sk(16), ("xs", s)],
                          writes=[("xs", s)])
                P.add("pool", CALL("dma_start", out=y_d[b * 128:(b + 1) * 128, :], in_=X[:]),
                      reads=[("xs", s)], writes=[("y", b)], dma_key=f"ys{s}")
                owner[s] = None
                yield "out"

        b_done = [-1]
        c_fin = [-1]
        a_loaded = [-1]

        def ab_stream():
            BLAG = 3
            a_next, a_done, b_next = 0, -1, 0
            ga = None
            bq = []
            turn = 0
            while b_done[0] < NBLK - 1:
                if ga is None and a_next < NBLK and a_next - 3 <= b_done[0]:
                    if a_loaded[0] < a_next:
                        load_A(a_next)
                        a_loaded[0] = a_next
                    ga = gen_A(a_next)
                    if a_next + 1 < NBLK and a_loaded[0] < a_next + 1 and NXT >= 2:
                        load_A(a_next + 1)
                        a_loaded[0] = a_next + 1
                if len(bq) < 2 and b_next < NBLK and a_done >= min(b_next + 1, NBLK - 1):
                    wi_b, bi_b = win_of[b_next]
                    need_c = wi_b - 2 if bi_b == wins[wi_b][1] - 1 else wi_b - 3
                    ok = c_fin[0] >= need_c and (not bq or bq[0][2] >= 1 + BLAG)
                    if ok and (b_next in slot_of or alloc_B(b_next)):
                        bq.append([gen_B(b_next), b_next, 0])
                        b_next += 1
                cands = []
                if ga is not None:
                    cands.append("A")
                if bq:
                    cands.append("B0")
                if len(bq) == 2 and bq[0][2] >= bq[1][2] + 1 + BLAG:
                    cands.append("B1")
                if not cands:
                    yield "blocked"
                    continue
                kind = cands[turn % len(cands)]
                turn += 1
                if kind == "A":
                    try:
                        next(ga)
                        yield "step"
                    except StopIteration:
                        a_done = a_next
                        a_next += 1
                        ga = None
                else:
                    ent = bq[0] if kind == "B0" else bq[1]
                    try:
                        next(ent[0])
                        ent[2] += 1
                        yield "step"
                    except StopIteration:
                        assert kind == "B0"
                        b_done[0] = ent[1]
                        bq.pop(0)
                        if bq:
                            pass

        def ready(wi):
            b0, nb = wins[wi]
            last_needed = b0 + nb if wi < NWIN - 1 else b0 + nb - 1
            return b_done[0] >= last_needed

        ab = ab_stream()
        ab_fin = False
        cgen = None
        next_win = 0
        credit = 0.0
        while True:
            if cgen is None and next_win < NWIN and ready(next_win):
                cgen = gen_C(next_win)
                next_win += 1
            if cgen is not None:
                try:
                    tag = next(cgen)
                except StopIteration:
                    cgen = None
                    continue
                credit += R_OUT if tag == "out" else R_PAIR
                while credit >= 1.0 and not ab_fin:
                    t = next(ab, "done")
                    if t == "done":
                        ab_fin = True
                    elif t == "blocked":
                        credit = 0.0
                        break
                    else:
                        credit -= 1.0
            else:
                if ab_fin:
                    assert next_win == NWIN
                    break
                t = next(ab, "done")
                if t == "done":
                    ab_fin = True
                else:
                    assert t != "blocked", "scheduler deadlock: AB stream blocked with no active C window"
        P.add("sp", CALL("nop"), reads=[("y", b) for b in range(NBLK)])
        P.emit(nc, st)
    return nc


def _rope_tables(positions):
    inv_freq = (np.float32(500000.0) ** (-(np.arange(0, 16, 2, dtype=np.float32)) / np.float32(16))).astype(np.float32)
    ang = positions.astype(np.float32)[:, None] * inv_freq[None, :]
    return np.cos(ang).astype(np.float32), np.sin(ang).astype(np.float32)


def prepare_weights(inp):
    f = lambda a: np.asarray(a, dtype=np.float32)
    w_in = f(inp["w_in"])[0]
    perm = []
    for g in range(4):
        perm += list(range(g * 64, g * 64 + 64)) + list(range((4 + g) * 64, (4 + g) * 64 + 64))
    cols = np.concatenate([np.array(perm), np.arange(512, DIN)])
    w_in = w_in[:, cols]
    w_in_l = np.ascontiguousarray(w_in.reshape(8, 128, DIN).transpose(1, 0, 2))
    w_o_l = np.ascontiguousarray(f(inp["w_o"])[0].reshape(8, 128, D).transpose(1, 0, 2))
    wfi = f(inp["w_ffn_in"])[0]
    gate = wfi[:, :DFF].reshape(8, 128, NPAIR, 128)
    up = wfi[:, DFF:].reshape(8, 128, NPAIR, 128)
    w_fi_l = np.ascontiguousarray(np.concatenate([gate, up], axis=3).transpose(1, 0, 2, 3))
    w_d_l = np.ascontiguousarray(f(inp["w_ffn_out"])[0].reshape(NPAIR, 128, D).transpose(1, 0, 2))
    w_s_l = np.ascontiguousarray(f(inp["gmlp_w_s"])[0].transpose(2, 0, 1))
    cp = np.zeros((128, C_TOT), np.float32)
    cp[:, C_G1:C_G1 + 8] = f(inp["norm_mix_pre"])[0].reshape(8, 128).T
    gmix = np.concatenate([f(inp["out_norm_attn"])[0], f(inp["out_norm_gmlp"])[0]])
    cp[:, C_GMIX:C_GMIX + 8] = gmix.reshape(8, 128).T
    cp[:, C_G2:C_G2 + 8] = f(inp["norm_ffn_pre"])[0].reshape(8, 128).T
    cp[:, C_SINK:C_SINK + 8] = f(inp["attn_sink"])[0][None, :]
    cp[:, C_BS:C_BS + 8] = f(inp["gmlp_b_s"])[0].T
    cw = f(inp["conv_w"])[0]
    cbias = f(inp["conv_b"])[0]
    allc = np.concatenate([cw, cbias[None, :]], axis=0)
    a = allc.reshape(4, 2, NPAIR, 128).transpose(3, 0, 1, 2).reshape(128, 4 * 2 * NPAIR)
    cp[:, C_CW:C_CW + 176] = a
    cp[:, C_NH:C_NH + 8] = -0.5
    cp[:, C_LNG:C_LNG + 512] = f(inp["gmlp_ln_g"])[0][None, :]
    cp[:, C_LNB:C_LNB + 512] = f(inp["gmlp_ln_b"])[0][None, :]
    cp[:, C_GPOST:C_GPOST + D] = f(inp["norm_mix_post"])[0][None, :]
    cp[:, C_G3:C_G3 + D] = f(inp["norm_ffn_post"])[0][None, :]
    cb = np.zeros((128, 640), np.float32)
    cb[:, 0:128] = np.eye(128)
    jj = np.arange(128)[:, None]
    ii = np.arange(128)[None, :]
    visP = (jj >= ii)
    visN = (jj <= ii)
    cb[:, 128:256] = np.where(visP, 0.0, -30000.0)
    cb[:, 256:384] = np.where(visN, 0.0, -30000.0)
    cb[:, 384:512] = visP
    cb[:, 512:640] = visN
    cb = cb.astype(ml_dtypes.bfloat16)
    return dict(w_in=w_in_l, w_o=w_o_l, w_fi=w_fi_l, w_d=w_d_l, w_s=w_s_l), cp, cb


def core_maps(units_per_core, cont_flags, weights, cp, cb, UB):
    maps = []
    for units, cont in zip(units_per_core, cont_flags):
        x = np.ascontiguousarray(np.concatenate(units, axis=0).astype(np.float32))
        L = UB * 128
        if cont:
            pos = np.concatenate([np.arange(2 * L), np.arange(L)])
        else:
            pos = np.concatenate([np.arange(L)] * 3)
        c, s = _rope_tables(pos)
        rope = np.concatenate([c.reshape(3 * UB, 128, 8), s.reshape(3 * UB, 128, 8)], axis=2).transpose(1, 0, 2)
        cpc = cp.copy()
        cpc[:, C_FLAG] = 1.0 if cont else 0.0
        m = dict(weights)
        m.update(x=x, cpack=cpc, rope=np.ascontiguousarray(rope.astype(np.float32)), cbf=cb)
        maps.append(m)
    return maps


_NC_CACHE = {}


def kernel(**inputs):
    UB = 32
    xp = np.asarray(inputs["x_prompt"], dtype=np.float32)
    xsm = np.asarray(inputs["x_sample"], dtype=np.float32)
    weights, cp, cb = prepare_weights(inputs)
    units, flags = [], []
    for c in range(4):
        units.append([xp[c, :4096], xp[c, 4096:], xsm[c]])
        flags.append(True)
    for c in range(4):
        units.append([xsm[4 + 3 * c + i] for i in range(3)])
        flags.append(False)
    maps = core_maps(units, flags, weights, cp, cb, UB)
    if UB not in _NC_CACHE:
        _NC_CACHE[UB] = build_program(UB)
    nc = _NC_CACHE[UB]
    res = run_bass_kernel_spmd(nc, maps, core_ids=list(range(8)))
    yp = np.empty((4, 8192, D), np.float32)
    ysm = np.empty((16, 4096, D), np.float32)
    for c in range(4):
        y = res.results[c]["y"]
        yp[c] = y[:8192]
        ysm[c] = y[8192:]
    for c in range(4):
        y = res.results[4 + c]["y"]
        for i in range(3):
            ysm[4 + 3 * c + i] = y[i * 4096:(i + 1) * 4096]
    return (yp, ysm)
```
